# Optimizing a Trainium2 kernel written in Bass

```python
import jax, jax.numpy as jnp
from jax import lax
import numpy as np

D_MODEL = 2048
BATCH = 4
SEQ = 4096
DEPTH = 1

HEAD_DIM = 128
RET_HEADS = 8
GDN_HEADS = 8
RET_WIDTH = RET_HEADS * HEAD_DIM
GDN_WIDTH = GDN_HEADS * HEAD_DIM
MIX_WIDTH = RET_WIDTH + GDN_WIDTH
D_FF = 5632
CHUNK = 64
CONV_K = 4
ROPE_THETA = 10000.0
EPS = 1e-6
N_IN = 4 * RET_WIDTH + 4 * GDN_WIDTH + 2 * GDN_HEADS

kernel_name = "hybrid_retention_gdn_macaron"


def _rmsnorm(x, w):
    xf = x.astype(jnp.float32)
    y = xf * lax.rsqrt(jnp.mean(xf * xf, axis=-1, keepdims=True) + EPS)
    return (y * w.astype(jnp.float32)).astype(x.dtype)


def _swiglu(x, w_gate, w_up, w_down):
    return (jax.nn.silu(x @ w_gate) * (x @ w_up)) @ w_down


def _rotary(x, positions):
    half = HEAD_DIM // 2
    inv_freq = ROPE_THETA ** (-jnp.arange(half, dtype=jnp.float32) / half)
    ang = positions.astype(jnp.float32)[..., None] * inv_freq
    cos = jnp.cos(ang)[:, :, None, :]
    sin = jnp.sin(ang)[:, :, None, :]
    x1, x2 = x[..., :half], x[..., half:]
    return jnp.concatenate([x1 * cos - x2 * sin, x2 * cos + x1 * sin], axis=-1)


def _l2norm(x):
    return x * lax.rsqrt(jnp.sum(x * x, axis=-1, keepdims=True) + EPS)


def _to_chunks(x):
    b, t, h = x.shape[:3]
    x = x.reshape((b, t // CHUNK, CHUNK, h) + x.shape[3:])
    return jnp.moveaxis(x, 3, 1)


def _from_chunks(x):
    b, h, n, c, d = x.shape
    return jnp.moveaxis(x, 1, 3).reshape(b, n * c, h, d)


def _retention(q, k, v):
    idx = jnp.arange(CHUNK, dtype=jnp.float32)
    log_gamma = jnp.log(1.0 - jnp.exp2(-5.0 - jnp.arange(RET_HEADS, dtype=jnp.float32)))
    rel = idx[:, None] - idx[None, :]
    causal = rel >= 0
    decay = jnp.where(causal, jnp.exp(log_gamma[:, None, None] * jnp.where(causal, rel, 0.0)), 0.0)
    scores = jnp.einsum('bhncd,bhnsd->bhncs', q, k) * decay[None, :, None]
    o_intra = jnp.einsum('bhncs,bhnse->bhnce', scores, v)
    q_dec = q * jnp.exp(log_gamma[:, None] * (idx + 1.0))[None, :, None, :, None]
    k_dec = k * jnp.exp(log_gamma[:, None] * (CHUNK - 1.0 - idx))[None, :, None, :, None]
    kv = jnp.einsum('bhncd,bhnce->bhnde', k_dec, v)
    chunk_decay = jnp.exp(log_gamma * CHUNK)[None, :, None, None]

    def step(state, inp):
        qd, kv_n = inp
        out = jnp.einsum('bhcd,bhde->bhce', qd, state)
        return chunk_decay * state + kv_n, out

    b, h, n, c, d = q.shape
    init = jnp.zeros((b, h, d, v.shape[-1]), jnp.float32)
    _, o_inter = lax.scan(step, init, (jnp.moveaxis(q_dec, 2, 0), jnp.moveaxis(kv, 2, 0)))
    return o_intra + jnp.moveaxis(o_inter, 0, 2)


def _gated_delta(q, k, v, g, beta):
    idx = jnp.arange(CHUNK)
    causal = idx[:, None] >= idx[None, :]
    strict = idx[:, None] > idx[None, :]
    gc = jnp.cumsum(g, axis=-1)
    decay = jnp.exp(jnp.where(causal, gc[..., :, None] - gc[..., None, :], -jnp.inf))
    k_beta = k * beta[..., None]
    m = jnp.where(strict, jnp.einsum('bhncd,bhnsd->bhncs', k_beta, k) * decay, 0.0)
    a = m + jnp.eye(CHUNK, dtype=jnp.float32)
    rhs = jnp.concatenate([v * beta[..., None], k_beta * jnp.exp(gc)[..., None]], axis=-1)
    sol = lax.linalg.triangular_solve(a, rhs, left_side=True, lower=True, unit_diagonal=True)
    dv = v.shape[-1]
    u, w = sol[..., :dv], sol[..., dv:]
    attn = jnp.einsum('bhncd,bhnsd->bhncs', q, k) * decay
    q_dec = q * jnp.exp(gc)[..., None]
    k_dec = k * jnp.exp(gc[..., -1:] - gc)[..., None]
    chunk_decay = jnp.exp(gc[..., -1])

    def step(state, inp):
        u_n, w_n, qd, kd, at, cd = inp
        v_new = u_n - jnp.einsum('bhck,bhkv->bhcv', w_n, state)
        out = jnp.einsum('bhck,bhkv->bhcv', qd, state) + jnp.einsum('bhcs,bhsv->bhcv', at, v_new)
        state = state * cd[..., None, None] + jnp.einsum('bhck,bhcv->bhkv', kd, v_new)
        return state, out

    b, h, n, c, dk = q.shape
    init = jnp.zeros((b, h, dk, dv), jnp.float32)
    xs = tuple(jnp.moveaxis(z, 2, 0) for z in (u, w, q_dec, k_dec, attn, chunk_decay))
    _, out = lax.scan(step, init, xs)
    return jnp.moveaxis(out, 0, 2)


def _mixer(h, positions, w_in, conv_w, a_log, dt_bias, gdn_norm_w, w_out):
    b, t, _ = h.shape
    f32 = jnp.float32
    proj = h @ w_in
    o1 = 4 * RET_WIDTH
    o2 = o1 + 3 * GDN_WIDTH
    o3 = o2 + GDN_WIDTH
    o4 = o3 + GDN_HEADS
    ret, gqkv, gz, ga, gb = jnp.split(proj, [o1, o2, o3, o4], axis=-1)

    def heads(z, n):
        return z.reshape(b, t, n, HEAD_DIM).astype(f32)

    rq, rk, rv, rg = jnp.split(ret, 4, axis=-1)
    rq = _rotary(heads(rq, RET_HEADS), positions)
    rk = _rotary(heads(rk, RET_HEADS), positions) * HEAD_DIM ** -0.5
    rv = heads(rv, RET_HEADS)
    ro = _from_chunks(_retention(_to_chunks(rq), _to_chunks(rk), _to_chunks(rv)))
    mu = jnp.mean(ro, axis=-1, keepdims=True)
    var = jnp.mean(jnp.square(ro - mu), axis=-1, keepdims=True)
    ro = (ro - mu) * lax.rsqrt(var + EPS)
    ro = ro.reshape(b, t, RET_WIDTH) * jax.nn.silu(rg.astype(f32))

    gqkv = jax.nn.silu(lax.conv_general_dilated(
        gqkv, conv_w.astype(gqkv.dtype)[:, None, :], window_strides=(1,),
        padding=[(CONV_K - 1, 0)], dimension_numbers=('NWC', 'WIO', 'NWC'),
        feature_group_count=3 * GDN_WIDTH))
    gq, gk, gv = jnp.split(gqkv, 3, axis=-1)
    gq = _l2norm(heads(gq, GDN_HEADS)) * HEAD_DIM ** -0.5
    gk = _l2norm(heads(gk, GDN_HEADS))
    gv = heads(gv, GDN_HEADS)
    g = -jnp.exp(a_log.astype(f32)) * jax.nn.softplus(ga.astype(f32) + dt_bias.astype(f32))
    beta = jax.nn.sigmoid(gb.astype(f32))
    go = _from_chunks(_gated_delta(_to_chunks(gq), _to_chunks(gk), _to_chunks(gv),
                                   _to_chunks(g), _to_chunks(beta)))
    go = go * lax.rsqrt(jnp.mean(go * go, axis=-1, keepdims=True) + EPS) * gdn_norm_w.astype(f32)
    go = (go * jax.nn.silu(heads(gz, GDN_HEADS))).reshape(b, t, GDN_WIDTH)

    mixed = jnp.concatenate([ro, go], axis=-1).astype(h.dtype)
    return mixed @ w_out


def setup_inputs(seed: int = 0) -> dict:
    key = jax.random.key(seed)
    ks = jax.random.split(key, 20)
    f32 = jnp.float32

    def nrm(k, shape, fan_in):
        return jax.random.normal(k, shape, f32) * fan_in ** -0.5

    def gain(k, shape):
        return 1.0 + 0.02 * jax.random.normal(k, shape, f32)

    x = jax.random.normal(ks[0], (BATCH, SEQ, D_MODEL), f32)
    positions = jnp.broadcast_to(jnp.arange(SEQ, dtype=jnp.int32), (BATCH, SEQ))
    dt = jnp.exp(jax.random.uniform(ks[10], (DEPTH, GDN_HEADS), f32, np.log(1e-3), np.log(1e-1)))
    return {
        "x": x,
        "positions": positions,
        "norm_ffn1_w": gain(ks[1], (DEPTH, D_MODEL)),
        "ffn1_w_gate": nrm(ks[2], (DEPTH, D_MODEL, D_FF), D_MODEL),
        "ffn1_w_up": nrm(ks[3], (DEPTH, D_MODEL, D_FF), D_MODEL),
        "ffn1_w_down": nrm(ks[4], (DEPTH, D_FF, D_MODEL), D_FF),
        "norm_mix_w": gain(ks[5], (DEPTH, D_MODEL)),
        "w_in": nrm(ks[6], (DEPTH, D_MODEL, N_IN), D_MODEL),
        "conv_w": nrm(ks[7], (DEPTH, CONV_K, 3 * GDN_WIDTH), CONV_K),
        "gdn_a_log": jnp.log(jax.random.uniform(ks[8], (DEPTH, GDN_HEADS), f32, 1.0, 16.0)),
        "gdn_dt_bias": dt + jnp.log(-jnp.expm1(-dt)),
        "gdn_norm_w": gain(ks[9], (DEPTH, HEAD_DIM)),
        "w_out": nrm(ks[11], (DEPTH, MIX_WIDTH, D_MODEL), MIX_WIDTH),
        "norm_ffn2_w": gain(ks[12], (DEPTH, D_MODEL)),
        "ffn2_w_gate": nrm(ks[13], (DEPTH, D_MODEL, D_FF), D_MODEL),
        "ffn2_w_up": nrm(ks[14], (DEPTH, D_MODEL, D_FF), D_MODEL),
        "ffn2_w_down": nrm(ks[15], (DEPTH, D_FF, D_MODEL), D_FF),
        "norm_final_w": gain(ks[16], (D_MODEL,)),
    }


def reference(x, positions, norm_ffn1_w, ffn1_w_gate, ffn1_w_up, ffn1_w_down,
              norm_mix_w, w_in, conv_w, gdn_a_log, gdn_dt_bias, gdn_norm_w, w_out,
              norm_ffn2_w, ffn2_w_gate, ffn2_w_up, ffn2_w_down, norm_final_w):
    for l in range(DEPTH):
        x = x + 0.5 * _swiglu(_rmsnorm(x, norm_ffn1_w[l]), ffn1_w_gate[l], ffn1_w_up[l], ffn1_w_down[l])
        x = x + _mixer(_rmsnorm(x, norm_mix_w[l]), positions, w_in[l], conv_w[l],
                       gdn_a_log[l], gdn_dt_bias[l], gdn_norm_w[l], w_out[l])
        x = x + 0.5 * _swiglu(_rmsnorm(x, norm_ffn2_w[l]), ffn2_w_gate[l], ffn2_w_up[l], ffn2_w_down[l])
    return _rmsnorm(x, norm_final_w)
```

```python
from contextlib import ExitStack
import numpy as np
import concourse.bass as bass
import concourse.mybir as mybir
from concourse.bass_utils import run_bass_kernel_spmd

F32 = mybir.dt.float32
BF16 = mybir.dt.bfloat16
I32 = mybir.dt.int32
AF = mybir.ActivationFunctionType
ALU = mybir.AluOpType

COMPUTE = ("pe", "act", "dve", "pool")
ALLQ = ("pe", "act", "dve", "pool", "sp")

D = 2048
DFF = 5632
NFT = DFF // 128
HD = 128
NH = 8
NIN = 8208
TB = 512
EPS = 1e-6
TWO_PI = 6.283185307179586
PI = 3.141592653589793


class Res:
    __slots__ = ("name", "lw", "rd")

    def __init__(self, name):
        self.name = name
        self.lw = None
        self.rd = []


class Ins:
    __slots__ = ("q", "fn", "deps", "inc", "tok", "dma_sem")

    def __init__(self, q, fn, deps, tok, dma_sem=None):
        self.q = q
        self.fn = fn
        self.deps = deps
        self.inc = False
        self.tok = tok
        self.dma_sem = dma_sem


class Prog:
    def __init__(self, nc):
        self.nc = nc
        self.q = {e: [] for e in ALLQ}
        self.ins_by_tok = {}
        self.dma_counts = {}

    def _deps(self, r, w):
        deps = set()
        for res in r:
            if res.lw is not None:
                deps.add(res.lw)
        for res in w:
            if res.lw is not None:
                deps.add(res.lw)
            deps.update(res.rd)
        return deps

    def _commit(self, tok, r, w):
        for res in r:
            res.rd.append(tok)
        for res in w:
            res.lw = tok
            res.rd = []

    def _mark(self, deps):
        for d in deps:
            if d[0] in COMPUTE:
                self.ins_by_tok[d].inc = True

    def op(self, eng, fn, r=(), w=()):
        deps = self._deps(r, w)
        lst = self.q[eng]
        tok = (eng, len(lst))
        if eng == "pe":
            deps = {d for d in deps if d[0] != "pe"}
        ins = Ins(eng, fn, deps, tok)
        lst.append(ins)
        self.ins_by_tok[tok] = ins
        self._mark(deps)
        self._commit(tok, r, w)
        return tok

    def dma(self, q, sem, fn, r=(), w=()):
        deps = self._deps(r, w)
        cnt = self.dma_counts.get(sem, 0) + 1
        self.dma_counts[sem] = cnt
        tok = ("dma:" + sem, cnt)
        ins = Ins(q, fn, deps, tok, dma_sem=sem)
        self.q[q].append(ins)
        self._mark(deps)
        self._commit(tok, r, w)
        return tok

    def wait_all(self, q, toks):
        deps = set(toks)
        tok = (q, len(self.q[q]))
        ins = Ins(q, None, deps, tok)
        if q in COMPUTE:
            self.ins_by_tok[tok] = ins
        self.q[q].append(ins)
        self._mark(deps)

    def emit(self, stack):
        nc = self.nc
        sems = {}
        for e in COMPUTE:
            sems[e] = stack.enter_context(nc.semaphore("s_" + e))
        for name in self.dma_counts:
            sems["dma:" + name] = stack.enter_context(nc.semaphore("d_" + name))
        val = {}
        for e in COMPUTE:
            c = 0
            for ins in self.q[e]:
                if ins.dma_sem is not None or ins.fn is None:
                    continue
                if ins.inc:
                    c += 1
                    val[ins.tok] = c
        block = stack.enter_context(nc.Block())
        attr = {"pe": "tensor", "act": "scalar", "dve": "vector", "pool": "gpsimd", "sp": "sync"}

        def make(qname):
            lst = self.q[qname]

            def body(eng):
                waited = {}
                for ins in lst:
                    need = {}
                    for d in ins.deps:
                        v = val[d] if d[0] in COMPUTE else 16 * d[1]
                        if v > need.get(d[0], 0):
                            need[d[0]] = v
                    for k, v in need.items():
                        if v > waited.get(k, 0):
                            eng.wait_ge(sems[k], v)
                            waited[k] = v
                    if ins.fn is None:
                        continue
                    r = ins.fn(eng)
                    if ins.dma_sem is not None:
                        r.then_inc(sems["dma:" + ins.dma_sem], 16)
                    elif ins.inc:
                        r.then_inc(sems[ins.tok[0]], 1)
            return body

        for qname in ALLQ:
            if self.q[qname]:
                getattr(block, attr[qname])(make(qname))


C_IDENT = 0
C_ONES = 128
C_U = 256
C_NEGT = 384
C_OFFD = 512
C_PERM = 640
C_CM = 768
C_NEG2 = 896
C_RDM = 1024
C_RQD = C_RDM + 1024
C_RKD = C_RQD + 1024
C_INVF = C_RKD + 8
C_TOT = C_INVF + 1


def _consts():
    c = np.zeros((128, C_TOT), np.float32)
    idx = np.arange(128)
    c[:, C_IDENT:C_IDENT + 128] = np.eye(128)
    c[:, C_ONES:C_ONES + 128] = 1.0
    s = idx[:, None]
    cc = idx[None, :]
    c[:, C_U:C_U + 128] = (s <= cc)
    c[:, C_NEGT:C_NEGT + 128] = np.where(cc >= s, 0.0, -1e30)
    c[:, C_OFFD:C_OFFD + 128] = 1.0 - np.eye(128)
    c[:, C_NEG2:C_NEG2 + 128] = np.where(s > cc, 0.0, -1e30)
    perm = np.zeros((128, 128), np.float32)
    for i in range(64):
        perm[i + 64, i] = -1.0
        perm[i, i + 64] = 1.0
    c[:, C_PERM:C_PERM + 128] = perm
    c[:, C_CM:C_CM + 128] = np.eye(128) - 1.0 / 128
    lg = np.log(1.0 - np.exp2(-5.0 - np.arange(NH, dtype=np.float64)))
    sc = HD ** -0.5
    for h in range(NH):
        rel = (cc - s).astype(np.float64)
        c[:, C_RDM + h * 128:C_RDM + (h + 1) * 128] = np.where(cc >= s, np.exp(lg[h] * np.maximum(rel, 0)) * sc, 0.0)
        c[:, C_RQD + h * 128:C_RQD + (h + 1) * 128] = np.exp(lg[h] * (idx + 1.0))[None, :]
        c[:, C_RKD + h] = np.exp(lg[h] * (127.0 - idx)) * sc
    half = 64
    invf = (10000.0 ** (-(np.arange(half, dtype=np.float32)) / half)).astype(np.float32)
    c[:, C_INVF] = np.concatenate([invf, invf])
    gam128 = [float(np.exp(lg[h] * 128.0)) for h in range(NH)]
    return c, gam128


def build(NPRE, NMAIN, taps=(), stage=99, sub=99):
    nc = bass.Bass("TRN2", target_bir_lowering=False)
    NBLK = NPRE + NMAIN
    SEQ = NBLK * TB
    _, GAM128 = _consts()

    def din(name, shape, dt=F32):
        return nc.dram_tensor(name, shape, dt, kind="ExternalInput").ap()

    x = din("x", [SEQ, D])
    pos = din("pos", [SEQ], I32)
    consts_d = din("consts", [128, C_TOT])
    g1_d = din("norm_ffn1_w", [D])
    gm_d = din("norm_mix_w", [D])
    g2_d = din("norm_ffn2_w", [D])
    gf_d = din("norm_final_w", [D])
    w1g = din("ffn1_w_gate", [D, DFF])
    w1u = din("ffn1_w_up", [D, DFF])
    w1d = din("ffn1_w_down", [DFF, D])
    w2g = din("ffn2_w_gate", [D, DFF])
    w2u = din("ffn2_w_up", [D, DFF])
    w2d = din("ffn2_w_down", [DFF, D])
    win = din("w_in", [D, NIN])
    wout = din("w_out", [D, D])
    convw = din("conv_w", [4, 3 * 1024])
    alog_d = din("gdn_a_log", [NH])
    dtb_d = din("gdn_dt_bias", [NH])
    gnw_d = din("gdn_norm_w", [HD])
    out = nc.dram_tensor("out", [NMAIN * TB, D], F32, kind="ExternalOutput").ap()
    tap_d = {}
    for (tn, shape) in taps:
        tap_d[tn] = nc.dram_tensor("tap_" + tn, list(shape), F32, kind="ExternalOutput").ap()

    def dscr(name, n, width):
        return nc.dram_tensor(name, [n, 128, width], BF16, kind="Internal").ap()

    SC = {
        "f1g": (dscr("sc_f1g", 22, 4096), [Res("sc_f1g%d" % i) for i in range(22)], set()),
        "f1u": (dscr("sc_f1u", 22, 4096), [Res("sc_f1u%d" % i) for i in range(22)], set()),
        "f1d": (dscr("sc_f1d", 44, 2048), [Res("sc_f1d%d" % i) for i in range(44)], set()),
        "f2g": (dscr("sc_f2g", 22, 4096), [Res("sc_f2g%d" % i) for i in range(22)], set()),
        "f2u": (dscr("sc_f2u", 22, 4096), [Res("sc_f2u%d" % i) for i in range(22)], set()),
        "f2d": (dscr("sc_f2d", 44, 2048), [Res("sc_f2d%d" % i) for i in range(44)], set()),
        "win": (dscr("sc_win", 64, 2048), [Res("sc_win%d" % i) for i in range(64)], set()),
        "wo": (dscr("sc_wo", 16, 2048), [Res("sc_wo%d" % i) for i in range(16)], set()),
    }

    st = ExitStack()
    with st:
        P = Prog(nc)

        def sb(name, shape, dt):
            return st.enter_context(nc.sbuf_tensor(name, shape, dt))

        def mm(out_, lhsT, rhs, r, w, start=True, stop=True):
            P.op("pe", lambda e: e.matmul(out_, lhsT=lhsT, rhs=rhs, start=start, stop=stop), r, w)

        def tr(out_, in_, ident, r, w):
            P.op("pe", lambda e: e.transpose(out_, in_, ident), r, w)

        def act(out_, in_, func, r, w, bias=None, scale=None, accum=None):
            kw = {}
            if bias is not None:
                kw["bias"] = bias
            if scale is not None:
                kw["scale"] = scale
            if accum is not None:
                kw["accum_out"] = accum
            P.op("act", lambda e: e.activation(out=out_, in_=in_, func=func, **kw), r, w)

        def tt(out_, in0, in1, op, r, w, eng="dve"):
            P.op(eng, lambda e: e.tensor_tensor(out=out_, in0=in0, in1=in1, op=op), r, w)

        def ts(out_, in0, s1, op0, r, w, s2=None, op1=None, eng="dve"):
            if op1 is None:
                P.op(eng, lambda e: e.tensor_scalar(out=out_, in0=in0, scalar1=s1, scalar2=None, op0=op0), r, w)
            else:
                P.op(eng, lambda e: e.tensor_scalar(out=out_, in0=in0, scalar1=s1, scalar2=s2, op0=op0, op1=op1), r, w)

        def stt(out_, in0, scalar, in1, op0, op1, r, w):
            P.op("dve", lambda e: e.scalar_tensor_tensor(out=out_, in0=in0, scalar=scalar, in1=in1, op0=op0, op1=op1), r, w)

        def cp(out_, in_, r, w, eng="dve"):
            if eng == "act":
                act(out_, in_, AF.Copy, r, w)
            else:
                P.op(eng, lambda e: e.tensor_copy(out=out_, in_=in_), r, w)

        def memset(ap, v, w):
            P.op("dve", lambda e: e.memset(ap, v), (), w)

        cnt = {"dma": 0}
        wo_toks = []

        def dma_small(q, out_, in_, w, r=()):
            cnt["dma"] += 1
            nm = "m%d" % cnt["dma"]
            return P.dma(q, nm, lambda e: e.dma_start(out=out_, in_=in_, allow_slow_non_contiguous=True), r, w)

        pb = [st.enter_context(nc.psum_tensor("pb%d" % i, [128, 512], F32)) for i in range(8)]
        pbb = [pb[i][:].bitcast(BF16) for i in range(8)]
        pr = [Res("pb%d" % i) for i in range(8)]
        bank_ctr = [0]

        def nextbank():
            i = bank_ctr[0] % 8
            bank_ctr[0] += 1
            return i

        cst = sb("cst", [128, 1033], F32); cst_r = Res("cst")
        cstb = sb("cstb", [128, 2048], BF16)
        identb = sb("identb", [128, 128], BF16); identb_r = Res("identb")
        xres = sb("xres", [128, 4, D], F32); xr = [Res("xres%d" % j) for j in range(4)]
        hT = sb("hT", [128, 16, TB], BF16); hT_r = [Res("hT%d" % k) for k in range(16)]
        big = sb("big", [128, NFT, TB], BF16); big_r = [Res("big%d" % k) for k in range(NFT)]

        class _XS:
            def __getitem__(self, idx):
                _, j, fs = idx
                return big[:, 4 * j:4 * j + 4, :].rearrange("p a b -> p (a b)")[:, fs]
        xs = _XS()
        xs_rl = [[big_r[4 * j + i] for i in range(4)] for j in range(4)]
        NRA = 4
        RA = [sb("ra%d" % i, [128, 16, 256], BF16) for i in range(NRA)]; ra_r = [Res("ra%d" % i) for i in range(NRA)]
        NRB = 3
        RB = [sb("rb%d" % i, [128, 2, 1024], BF16) for i in range(NRB)]; rb_r = [Res("rb%d" % i) for i in range(NRB)]
        ra_ctr = [0]
        rb_ctr = [0]
        gcols = sb("gcols", [128, 3, 16], F32); gcols_r = Res("gcols")
        gfin = sb("gfin", [128, D], F32); gfin_r = Res("gfin")
        ss = sb("ss", [128, 4], F32); ss_r = Res("ss")
        rstd = sb("rstd", [128, 4], F32); rstd_r = Res("rstd")
        sg = [sb("sg%d" % i, [128, TB], BF16) for i in range(2)]; sg_r = [Res("sg%d" % i) for i in range(2)]
        Sret = sb("Sret", [128, NH, 128], F32); Sret_r = [Res("Sret%d" % h) for h in range(NH)]
        Sretb = sb("Sretb", [128, NH, 128], BF16); Sretb_r = [Res("Sretb%d" % h) for h in range(NH)]
        Sg = sb("Sg", [128, NH, 128], F32); Sg_r = [Res("Sg%d" % h) for h in range(NH)]
        Sgb = sb("Sgb", [128, NH, 128], BF16); Sgb_r = [Res("Sgb%d" % h) for h in range(NH)]
        hist = sb("hist", [128, 24, 4], F32); hist_r = [Res("hist%d" % g) for g in range(24)]
        cwT = sb("cwT", [128, 24, 4], F32); cwT_r = Res("cwT")
        arow = sb("arow", [128, 3, NH], F32); arow_r = Res("arow")
        gnw = sb("gnw", [128, 1], F32); gnw_r = Res("gnw")
        cosT = sb("cosT", [128, TB], F32); cosT_r = Res("cosT")
        sinT = sb("sinT", [128, TB], F32); sinT_r = Res("sinT")
        NF = 5
        Fs = [sb("F%d" % i, [128, TB + 4], F32) for i in range(NF)]; Fs_r = [Res("F%d" % i) for i in range(NF)]
        f_ctr = [0]

        def nextF():
            i = f_ctr[0] % NF
            f_ctr[0] += 1
            return Fs[i], Fs_r[i]

        big_alias = {}
        SF_t, SF_r = [], []
        for k in range(16, 36):
            v = big[:, k, :].bitcast(F32)
            big_alias[k] = []
            for t in range(2):
                SF_t.append(v[:, t * 128:(t + 1) * 128])
                rr = Res("sfb%d_%d" % (k, t))
                SF_r.append(rr)
                big_alias[k].append(rr)
        NSF = len(SF_t)
        sf_ctr = [0]

        def nextSF():
            i = sf_ctr[0] % NSF
            sf_ctr[0] += 1
            return SF_t[i], SF_r[i]

        SB_t, SB_r = [], []
        for k in range(36, 44):
            big_alias[k] = []
            for t in range(4):
                SB_t.append(big[:, k, t * 128:(t + 1) * 128])
                rr = Res("sbb%d_%d" % (k, t))
                SB_r.append(rr)
                big_alias[k].append(rr)
        NSB = len(SB_t)
        sbt_ctr = [0]

        def nextSB():
            i = sbt_ctr[0] % NSB
            sbt_ctr[0] += 1
            return SB_t[i], SB_r[i]

        qn = sb("qn", [128, TB], BF16); qn_r = Res("qn")
        kn = sb("kn", [128, TB], BF16); kn_r = Res("kn")
        vsb = sb("vsb", [128, TB], BF16); vsb_r = Res("vsb")
        zs = sb("zs", [128, TB], F32); zs_r = Res("zs")
        ktok = sb("ktok", [128, 3, 4, 128], BF16); ktok_r = [Res("ktok%d" % i) for i in range(3)]
        osb = sb("osb", [128, TB], F32); osb_r = Res("osb")
        qdec = sb("qdec", [128, TB], BF16); qdec_r = Res("qdec")
        GN = 11
        gt = sb("gt", [128, GN, 4, NH], F32); gt_r = Res("gt")
        G_A, G_B, G_G, G_BETA, G_GC, G_NGC, G_EGC, G_EKD, G_ECD, G_BG, G_NBETA = range(GN)
        gtmp = sb("gtmp", [128, 4, 4, NH], F32); gtmp_r = Res("gtmp")

        dma_small("sp", cst[:, 0:1024], consts_d[:, 0:1024], [cst_r])
        dma_small("sp", cst[:, 1024:1033], consts_d[:, C_RKD:C_TOT], [cst_r])
        dma_small("pool", cstb[:], consts_d[:, C_RDM:C_RDM + 2048], [cst_r])
        cident = cst[:, C_IDENT:C_IDENT + 128]
        cones = cst[:, C_ONES:C_ONES + 128]
        cU = cst[:, C_U:C_U + 128]
        cnegt = cst[:, C_NEGT:C_NEGT + 128]
        coffd = cst[:, C_OFFD:C_OFFD + 128]
        cperm = cst[:, C_PERM:C_PERM + 128]
        ccm = cst[:, C_CM:C_CM + 128]
        cinvf = cst[:, 1032:1033]
        cneg2 = cst[:, C_NEG2:C_NEG2 + 128]
        cp(identb[:], cident, [cst_r], [identb_r])
        for i, gd in enumerate((g1_d, gm_d, g2_d)):
            dma_small("sp", gcols[:, i, :], gd.rearrange("(kt p) -> p kt", p=128), [gcols_r])
        dma_small("sp", gfin[:], gf_d.partition_broadcast(128), [gfin_r])
        for j in range(4):
            dma_small("sp", cwT[:, :, j], convw[j, :].rearrange("(g p) -> p g", p=128), [cwT_r])
        dma_small("sp", arow[:, 2, :], alog_d.partition_broadcast(128), [arow_r])
        dma_small("sp", arow[:, 1, :], dtb_d.partition_broadcast(128), [arow_r])
        dma_small("sp", gnw[:], gnw_d.rearrange("(p o) -> p o", o=1), [gnw_r])
        wabf = Fs[0][:, 0:256].rearrange("p (a b) -> p a b", a=16)
        wab = sb("wab", [128, 16, 16], BF16); wab_r = Res("wab")
        dma_small("sp", wabf, win.rearrange("(kt p) n -> p kt n", p=128)[:, :, 8192:8208], [Fs_r[0]])
        cp(wab[:], wabf, [Fs_r[0]], [wab_r])
        act(arow[:, 0, :], arow[:, 2, :], AF.Exp, [arow_r], [arow_r])
        ts(arow[:, 0, :], arow[:, 0, :], -1.0, ALU.mult, [arow_r], [arow_r])
        for h in range(NH):
            memset(Sret[:, h, :], 0.0, [Sret_r[h]])
            memset(Sretb[:, h, :], 0.0, [Sretb_r[h]])
            memset(Sg[:, h, :], 0.0, [Sg_r[h]])
            memset(Sgb[:, h, :], 0.0, [Sgb_r[h]])
        for g in range(24):
            memset(hist[:, g, :], 0.0, [hist_r[g]])

        def tap(name, ap, r):
            if name in tap_d:
                dma_small("sp", tap_d[name], ap, [], r=r)

        def load_x(blk):
            t0 = blk * TB
            for j in range(4):
                P.dma("sp", "x%d" % j,
                      lambda e, j=j: e.dma_start(out=xres[:, j, :], in_=x[t0 + j * 128:t0 + (j + 1) * 128, :]),
                      (), [xr[j]])

        def rms_stats():
            for j in range(4):
                act(xs[:, j, :], xres[:, j, :], AF.Square, [xr[j]], xs_rl[j] + [ss_r], accum=ss[:, j:j + 1])
            ts(rstd[:], ss[:], 1.0 / D, ALU.mult, [ss_r], [rstd_r], s2=EPS, op1=ALU.add)
            act(rstd[:], rstd[:], AF.Ln, [rstd_r], [rstd_r])
            act(rstd[:], rstd[:], AF.Exp, [rstd_r], [rstd_r], scale=-0.5)

        def norm_to_hT(gi):
            rms_stats()
            for j in range(4):
                ts(xs[:, j, :], xres[:, j, :], rstd[:, j:j + 1], ALU.mult, [xr[j], rstd_r], xs_rl[j])
            for kt in range(16):
                bk = nextbank()
                pv = pbb[bk]
                for j in range(4):
                    tr(pv[:, j * 128:(j + 1) * 128], xs[:, j, kt * 128:(kt + 1) * 128], identb[:],
                       xs_rl[j] + [identb_r], [pr[bk]])
                if kt % 2 == 0:
                    ts(hT[:, kt, :], pv[:, 0:TB], gcols[:, gi, kt:kt + 1], ALU.mult, [pr[bk], gcols_r], [hT_r[kt]])
                else:
                    act(hT[:, kt, :], pv[:, 0:TB], AF.Copy, [pr[bk], gcols_r], [hT_r[kt]], scale=gcols[:, gi, kt:kt + 1])

        def load_ra(src_ap, ncols, ck=None):
            i = ra_ctr[0] % NRA
            ra_ctr[0] += 1
            dst = RA[i][:, :, 0:ncols]
            if ck is None:
                P.dma("pool", "ra%d" % i, lambda e: e.dma_start(out=dst, in_=src_ap), (), [ra_r[i]])
                return i
            sct, resl, done = SC[ck[0]]
            idx = ck[1]
            scv = sct[idx, :, 0:16 * ncols].rearrange("p (k n) -> p k n", n=ncols)
            if idx not in done:
                done.add(idx)
                P.dma("pool", "ra%d" % i, lambda e: e.dma_start(out=dst, in_=src_ap), (), [ra_r[i]])
                wo_toks.append(P.dma("sp", "wra%d" % i, lambda e: e.dma_start(out=scv, in_=dst), [ra_r[i]], [resl[idx]]))
            else:
                P.dma("pool", "ra%d" % i, lambda e: e.dma_start(out=dst, in_=scv), [resl[idx]], [ra_r[i]])
            return i

        def colview(wap, c0, ncols):
            return wap.rearrange("(kt p) n -> p kt n", p=128)[:, :, c0:c0 + ncols]

        def rowproj(src_r, src_tile, nk, wd_ap, scale, ckn):
            wv = wd_ap.rearrange("(ft p) m -> p ft m", p=128)
            for half in range(2):
                sl = None
                for f in range(nk):
                    if f % 2 == 0:
                        sl = rb_ctr[0] % NRB
                        rb_ctr[0] += 1
                        sct, resl, done = SC[ckn]
                        idx = half * (nk // 2) + f // 2
                        scv = sct[idx, :, :].rearrange("p (k n) -> p k n", n=1024)
                        if idx not in done:
                            done.add(idx)
                            P.dma("pool", "rb%d" % sl,
                                  lambda e, sl=sl, f=f, half=half: e.dma_start(out=RB[sl][:], in_=wv[:, f:f + 2, half * 1024:(half + 1) * 1024]),
                                  (), [rb_r[sl]])
                            wo_toks.append(P.dma("sp", "wrb%d" % sl, lambda e, sl=sl, scv=scv: e.dma_start(out=scv, in_=RB[sl][:]),
                                                 [rb_r[sl]], [resl[idx]]))
                        else:
                            P.dma("pool", "rb%d" % sl, lambda e, sl=sl, scv=scv: e.dma_start(out=RB[sl][:], in_=scv),
                                  [resl[idx]], [rb_r[sl]])
                    for j in range(4):
                        for mb in range(2):
                            b = j * 2 + mb
                            mm(pb[b][:], src_tile[:, f, j * 128:(j + 1) * 128], RB[sl][:, f % 2, mb * 512:(mb + 1) * 512],
                               [src_r[f], rb_r[sl]] + (big_alias.get(f, []) if nk == NFT else []), [pr[b]], start=(f == 0), stop=(f == nk - 1))
                for j in range(4):
                    for mb in range(2):
                        b = j * 2 + mb
                        col = half * 1024 + mb * 512
                        stt(xres[:, j, col:col + 512], pb[b][:], float(scale), xres[:, j, col:col + 512],
                            ALU.mult, ALU.add, [pr[b], xr[j]], [xr[j]])

        def ffn(wg, wu, wd, pfx):
            for fg in range(NFT // 2):
                sa = load_ra(colview(wg, fg * 256, 256), 256, ck=(pfx + "g", fg))
                su = load_ra(colview(wu, fg * 256, 256), 256, ck=(pfx + "u", fg))
                for fo in range(2):
                    f = fg * 2 + fo
                    bg = nextbank()
                    bu = nextbank()
                    for kt in range(16):
                        mm(pb[bg][:], RA[sa][:, kt, fo * 128:(fo + 1) * 128], hT[:, kt, :], [ra_r[sa], hT_r[kt]], [pr[bg]],
                           start=(kt == 0), stop=(kt == 15))
                        mm(pb[bu][:], RA[su][:, kt, fo * 128:(fo + 1) * 128], hT[:, kt, :], [ra_r[su], hT_r[kt]], [pr[bu]],
                           start=(kt == 0), stop=(kt == 15))
                    act(sg[f % 2][:], pb[bg][:], AF.Silu, [pr[bg]], [sg_r[f % 2]])
                    tt(big[:, f, :], sg[f % 2][:], pb[bu][:], ALU.mult, [sg_r[f % 2], pr[bu]], [big_r[f]] + big_alias.get(f, []))
            rowproj(big_r, big, NFT, wd, 0.5, pfx + "d")

        def proj_fm(c0, blk_main=True):
            sl = load_ra(colview(win, c0, 128), 128, ck=("win", c0 // 128))
            bk = nextbank()
            for kt in range(16):
                mm(pb[bk][:], RA[sl][:, kt, 0:128], hT[:, kt, :], [ra_r[sl], hT_r[kt]], [pr[bk]], start=(kt == 0), stop=(kt == 15))
            return bk

        def rsqrt_ln(out_ap, in_ap, r, w, scale, bias_exp=None):
            act(out_ap, in_ap, AF.Ln, r, w, bias=epsc[:, 0:1], scale=scale)
            if bias_exp is None:
                act(out_ap, out_ap, AF.Exp, w, w, scale=-0.5)
            else:
                act(out_ap, out_ap, AF.Exp, list(w) + [epsc_r], w, scale=-0.5, bias=bias_exp)

        epsc = sb("epsc", [128, 4], F32); epsc_r = Res("epsc")
        memset(epsc[:, 0:1], EPS, [epsc_r])
        memset(epsc[:, 1:2], float(-0.5 * np.log(128.0)), [epsc_r])
        memset(epsc[:, 2:3], 1.0, [epsc_r])
        memset(epsc[:, 3:4], 0.0, [epsc_r])

        def rotary_tables(blk):
            t0 = blk * TB
            pbuf, posi_r = nextF()
            posi = pbuf[:, 0:TB].bitcast(I32)
            P.dma("sp", "pos", lambda e: e.dma_start(out=posi, in_=pos[t0:t0 + TB].partition_broadcast(128)), (), [posi_r])
            ang, ang_r = nextF()
            cp(ang[:, 0:TB], posi, [posi_r], [ang_r])
            ts(ang[:, 0:TB], ang[:, 0:TB], cinvf, ALU.mult, [ang_r, cst_r], [ang_r])
            for (dst, dst_r, shift) in ((sinT, sinT_r, 0.0), (cosT, cosT_r, PI / 2)):
                a2, a2_r = nextF()
                t, t_r = nextF()
                m, m_r = nextF()
                ts(a2[:, 0:TB], ang[:, 0:TB], float(shift), ALU.add, [ang_r], [a2_r])
                ts(t[:, 0:TB], a2[:, 0:TB], float(1.0 / TWO_PI), ALU.mult, [a2_r], [t_r])
                ki = m[:, 0:TB].bitcast(I32)
                cp(ki, t[:, 0:TB], [t_r], [m_r])
                cp(t[:, 0:TB], ki, [m_r], [t_r])
                stt(a2[:, 0:TB], t[:, 0:TB], float(-TWO_PI), a2[:, 0:TB], ALU.mult, ALU.add, [t_r, a2_r], [a2_r])
                ts(m[:, 0:TB], a2[:, 0:TB], float(PI), ALU.is_gt, [a2_r], [m_r])
                stt(a2[:, 0:TB], m[:, 0:TB], float(-TWO_PI), a2[:, 0:TB], ALU.mult, ALU.add, [m_r, a2_r], [a2_r])
                ts(m[:, 0:TB], a2[:, 0:TB], float(-PI), ALU.is_lt, [a2_r], [m_r])
                stt(a2[:, 0:TB], m[:, 0:TB], float(TWO_PI), a2[:, 0:TB], ALU.mult, ALU.add, [m_r, a2_r], [a2_r])
                ts(a2[:, 0:TB], a2[:, 0:TB], float(PI), ALU.min, [a2_r], [a2_r], s2=float(-PI), op1=ALU.max)
                act(dst[:], a2[:, 0:TB], AF.Sin, [a2_r], [dst_r])

        def ret_head(h, main):
            bk_k = proj_fm(1024 + h * 128)
            kf, kf_r = nextF()
            cp(kf[:, 0:TB], pb[bk_k][:], [pr[bk_k]], [kf_r], eng="act")
            if main:
                bk_q = proj_fm(h * 128)
                qf, qf_r = nextF()
                cp(qf[:, 0:TB], pb[bk_q][:], [pr[bk_q]], [qf_r], eng="act")
                bk_g = proj_fm(3072 + h * 128)
                act(zs[:], pb[bk_g][:], AF.Silu, [pr[bk_g]], [zs_r])
            sl = load_ra(colview(win, 2048 + h * 128, 128), 128, ck=("win", 16 + h))
            bk_v = nextbank()
            for j in range(4):
                for kt in range(16):
                    mm(pb[bk_v][:, j * 128:(j + 1) * 128], hT[:, kt, j * 128:(j + 1) * 128], RA[sl][:, kt, 0:128],
                       [ra_r[sl], hT_r[kt]], [pr[bk_v]], start=(kt == 0), stop=(kt == 15))
            vt = ktok[:, 1, :, :]
            cp(vt.rearrange("p a b -> p (a b)"), pb[bk_v][:], [pr[bk_v]], [ktok_r[1]])

            def rot(srcf, srcf_r, dst, dst_r):
                bk = nextbank()
                mm(pb[bk][:], cperm, srcf[:, 0:TB], [cst_r, srcf_r], [pr[bk]])
                t1, t1_r = nextF()
                t2, t2_r = nextF()
                tt(t1[:, 0:TB], srcf[:, 0:TB], cosT[:], ALU.mult, [srcf_r, cosT_r], [t1_r])
                tt(t2[:, 0:TB], pb[bk][:], sinT[:], ALU.mult, [pr[bk], sinT_r], [t2_r])
                tt(dst[:], t1[:, 0:TB], t2[:, 0:TB], ALU.add, [t1_r, t2_r], [dst_r])

            rot(kf, kf_r, kn, kn_r)
            if main:
                rot(qf, qf_r, qn, qn_r)
                tt(qdec[:].rearrange("p (a b) -> p a b", a=4), qn[:].rearrange("p (a b) -> p a b", a=4),
                   cstb[:, 1024 + h * 128:1024 + (h + 1) * 128].unsqueeze(1).broadcast_to([128, 4, 128]),
                   ALU.mult, [qn_r, cst_r], [qdec_r])
            bk = nextbank()
            for j in range(4):
                tr(pbb[bk][:, j * 128:(j + 1) * 128], kn[:, j * 128:(j + 1) * 128], identb[:], [kn_r, identb_r], [pr[bk]])
            ts(ktok[:, 0, :, :].rearrange("p a b -> p (a b)"), pbb[bk][:, 0:TB], cst[:, 1024 + h:1024 + h + 1], ALU.mult,
               [pr[bk], cst_r], [ktok_r[0]])
            for j in range(4):
                cs = slice(j * 128, (j + 1) * 128)
                if main:
                    bs = nextbank()
                    mm(pb[bs][:, 0:128], kn[:, cs], qn[:, cs], [kn_r, qn_r], [pr[bs]])
                    scm, scm_r = nextSB()
                    tt(scm, pb[bs][:, 0:128], cstb[:, h * 128:(h + 1) * 128], ALU.mult, [pr[bs], cst_r], [scm_r])
                    bo = nextbank()
                    mm(pb[bo][:, 0:128], ktok[:, 1, j, :], scm, [ktok_r[1], scm_r], [pr[bo]], start=True, stop=False)
                    mm(pb[bo][:, 0:128], Sretb[:, h, :], qdec[:, cs], [Sretb_r[h], qdec_r], [pr[bo]], start=False, stop=True)
                    cp(osb[:, cs], pb[bo][:, 0:128], [pr[bo]], [osb_r], eng="act")
                bd = nextbank()
                mm(pb[bd][:, 0:128], ktok[:, 0, j, :], ktok[:, 1, j, :], [ktok_r[0], ktok_r[1]], [pr[bd]])
                stt(Sretb[:, h, :], Sret[:, h, :], float(GAM128[h]), pb[bd][:, 0:128], ALU.mult, ALU.add,
                    [Sret_r[h], pr[bd]], [Sretb_r[h]])
                stt(Sret[:, h, :], Sret[:, h, :], float(GAM128[h]), pb[bd][:, 0:128], ALU.mult, ALU.add,
                    [Sret_r[h], pr[bd]], [Sret_r[h]])
            if main:
                bc_ = nextbank()
                mm(pb[bc_][:], ccm, osb[:], [cst_r, osb_r], [pr[bc_]])
                sq, sq_r = nextF()
                act(sq[:, 0:TB], pb[bc_][:], AF.Square, [pr[bc_]], [sq_r])
                b2 = nextbank()
                mm(pb[b2][:], cones, sq[:, 0:TB], [cst_r, sq_r], [pr[b2]])
                rs, rs_r = nextF()
                rsqrt_ln(rs[:, 0:TB], pb[b2][:], [pr[b2], epsc_r], [rs_r], 1.0 / 128)
                t, t_r = nextF()
                tt(t[:, 0:TB], pb[bc_][:], rs[:, 0:TB], ALU.mult, [pr[bc_], rs_r], [t_r])
                tt(big[:, h, :], t[:, 0:TB], zs[:], ALU.mult, [t_r, zs_r], [big_r[h]])

        def gdn_gates():
            bk = nextbank()
            for j in range(4):
                for kt in range(16):
                    mm(pb[bk][:, j * 16:(j + 1) * 16], hT[:, kt, j * 128:(j + 1) * 128], wab[:, kt, :],
                       [wab_r, hT_r[kt]], [pr[bk]], start=(kt == 0), stop=(kt == 15))
            pv = pb[bk][:, 0:64].rearrange("p (j c) -> p j c", j=4)
            cp(gt[:, G_A, :, :], pv[:, :, 0:8], [pr[bk]], [gt_r])
            cp(gt[:, G_B, :, :], pv[:, :, 8:16], [pr[bk]], [gt_r])
            bro = lambda row: arow[:, row, :].unsqueeze(1).broadcast_to([128, 4, NH])
            R = [gt_r, gtmp_r, arow_r]
            xa = gtmp[:, 0]
            ax = gtmp[:, 1]
            l = gtmp[:, 2]
            tt(xa, gt[:, G_A], bro(1), ALU.add, R, [gtmp_r])
            stt(ax, xa, -1.0, xa, ALU.mult, ALU.max, R, [gtmp_r])
            act(l, ax, AF.Exp, R, [gtmp_r], scale=-1.0)
            act(l, l, AF.Ln, R + [epsc_r], [gtmp_r], bias=epsc[:, 2:3])
            ts(ax, xa, 0.0, ALU.max, R, [gtmp_r])
            tt(l, l, ax, ALU.add, R, [gtmp_r])
            tt(gt[:, G_G], l, bro(0), ALU.mult, R, [gt_r])
            act(l, gt[:, G_B], AF.Exp, R, [gtmp_r], scale=-1.0)
            ts(l, l, 1.0, ALU.add, R, [gtmp_r])
            P.op("dve", lambda e: e.reciprocal(out=gt[:, G_BETA], in_=l), R, [gt_r])
            bk2 = nextbank()
            ggf = gt[:, G_G, :, :].rearrange("p j c -> p (j c)")
            mm(pb[bk2][:, 0:32], cU, ggf, [cst_r, gt_r], [pr[bk2]])
            mm(pb[bk2][:, 64:96], cones, ggf, [cst_r, gt_r], [pr[bk2]])
            gcv = pb[bk2][:, 0:32].rearrange("p (j c) -> p j c", j=4)
            glv = pb[bk2][:, 64:96].rearrange("p (j c) -> p j c", j=4)
            cp(gt[:, G_GC], gcv, [pr[bk2]], [gt_r])
            ts(gt[:, G_NGC], gcv, -1.0, ALU.mult, [pr[bk2]], [gt_r])
            act(gt[:, G_EGC], gcv, AF.Exp, [pr[bk2]], [gt_r])
            act(gt[:, G_ECD], glv, AF.Exp, [pr[bk2]], [gt_r])
            tt(l, glv, gt[:, G_GC], ALU.subtract, [pr[bk2], gt_r, gtmp_r], [gtmp_r])
            act(gt[:, G_EKD], l, AF.Exp, [gtmp_r], [gt_r])
            tt(gt[:, G_BG], gt[:, G_BETA], gt[:, G_EGC], ALU.mult, [gt_r], [gt_r])
            ts(gt[:, G_NBETA], gt[:, G_BETA], -1.0, ALU.mult, [gt_r], [gt_r])

        def gdn_head(h, main, qhist=False):
            def conv_path(qi, c0, dst, dst_r, norm, qscale=False):
                g = qi * NH + h
                bk = proj_fm(c0)
                xc, xc_r = nextF()
                cp(xc[:, 0:3], hist[:, g, 0:3], [hist_r[g]], [xc_r])
                cp(xc[:, 3:3 + TB], pb[bk][:], [pr[bk]], [xc_r], eng="act")
                cp(hist[:, g, 0:3], xc[:, TB:TB + 3], [xc_r], [hist_r[g]])
                if sub < 0.1:
                    return
                y, y_r = nextF()
                ts(y[:, 0:TB], xc[:, 3:3 + TB], cwT[:, g, 3:4], ALU.mult, [xc_r, cwT_r], [y_r])
                for jj in (2, 1, 0):
                    stt(y[:, 0:TB], xc[:, jj:jj + TB], cwT[:, g, jj:jj + 1], y[:, 0:TB], ALU.mult, ALU.add,
                        [xc_r, cwT_r, y_r], [y_r])
                if sub < 0.2:
                    return
                if not norm:
                    act(dst[:], y[:, 0:TB], AF.Silu, [y_r], [dst_r])
                    return
                act(y[:, 0:TB], y[:, 0:TB], AF.Silu, [y_r], [y_r])
                sq, sq_r = nextF()
                act(sq[:, 0:TB], y[:, 0:TB], AF.Square, [y_r], [sq_r])
                if sub < 0.3:
                    return
                b2 = nextbank()
                mm(pb[b2][:], cones, sq[:, 0:TB], [cst_r, sq_r], [pr[b2]])
                if sub < 0.4:
                    return
                rsqrt_ln(sq[:, 0:TB], pb[b2][:], [pr[b2], epsc_r], [sq_r], 1.0, bias_exp=(epsc[:, 1:2] if qscale else None))
                if sub < 0.5:
                    return
                tt(dst[:], y[:, 0:TB], sq[:, 0:TB], ALU.mult, [y_r, sq_r], [dst_r])

            conv_path(1, 5120 + h * 128, kn, kn_r, True)
            conv_path(2, 6144 + h * 128, vsb, vsb_r, False)
            if main or qhist:
                conv_path(0, 4096 + h * 128, qn, qn_r, True, qscale=True)
            if main:
                bz = proj_fm(7168 + h * 128)
                act(zs[:], pb[bz][:], AF.Silu, [pr[bz]], [zs_r])
            if sub < 1:
                return
            bk = nextbank()
            for j in range(4):
                tr(pbb[bk][:, j * 128:(j + 1) * 128], kn[:, j * 128:(j + 1) * 128], identb[:], [kn_r, identb_r], [pr[bk]])
            bkv = nextbank()
            for j in range(4):
                tr(pbb[bkv][:, j * 128:(j + 1) * 128], vsb[:, j * 128:(j + 1) * 128], identb[:], [vsb_r, identb_r], [pr[bkv]])
            for j in range(4):
                ksl = pbb[bk][:, j * 128:(j + 1) * 128]
                vsl = pbb[bkv][:, j * 128:(j + 1) * 128]
                ts(ktok[:, 0, j, :], ksl, gt[:, G_BG, j, h:h + 1], ALU.mult, [pr[bk], gt_r], [ktok_r[0]])
                ts(ktok[:, 1, j, :], ksl, gt[:, G_EKD, j, h:h + 1], ALU.mult, [pr[bk], gt_r], [ktok_r[1]])
                ts(ktok[:, 2, j, :], vsl, gt[:, G_BETA, j, h:h + 1], ALU.mult, [pr[bkv], gt_r], [ktok_r[2]])

            if sub < 2:
                return
            prep = {}
            for pair in range(1):
                js = (0, 1, 2, 3)
                stt_ = {}
                for j in js:
                    cs = slice(j * 128, (j + 1) * 128)
                    d = {}
                    bkk = nextbank()
                    mm(pb[bkk][:, 0:128], kn[:, cs], kn[:, cs], [kn_r], [pr[bkk]])
                    if main:
                        bqk = nextbank()
                        mm(pb[bqk][:, 0:128], kn[:, cs], qn[:, cs], [kn_r, qn_r], [pr[bqk]])
                    dg, dg_r = nextSF()
                    db, db_r = nextSF()
                    ts(dg, cU, gt[:, G_G, j, h:h + 1], ALU.mult, [cst_r, gt_r], [dg_r])
                    ts(db, cident, gt[:, G_BETA, j, h:h + 1], ALU.mult, [cst_r, gt_r], [db_r])
                    bgc = nextbank()
                    mm(pb[bgc][:, 0:128], cones, dg, [cst_r, dg_r], [pr[bgc]])
                    bbe = nextbank()
                    mm(pb[bbe][:, 0:128], cones, db, [cst_r, db_r], [pr[bbe]])
                    dec, dec_r = nextSF()
                    tt(dec, pb[bgc][:, 0:128], cnegt, ALU.add, [pr[bgc], cst_r], [dec_r])
                    act(dec, dec, AF.Exp, [dec_r, gt_r], [dec_r], bias=gt[:, G_NGC, j, h:h + 1])
                    dec2, dec2_r = nextSF()
                    stt(dec2, pb[bgc][:, 0:128], -1.0, cneg2, ALU.mult, ALU.add, [pr[bgc], cst_r], [dec2_r])
                    act(dec2, dec2, AF.Exp, [dec2_r, gt_r], [dec2_r], bias=gt[:, G_GC, j, h:h + 1])
                    if main:
                        at, at_r = nextSB()
                        tt(at, pb[bqk][:, 0:128], dec, ALU.mult, [pr[bqk], dec_r], [at_r])
                        eg, eg_r = nextSF()
                        act(eg, pb[bgc][:, 0:128], AF.Exp, [pr[bgc]], [eg_r])
                        qd, qd_r = nextSB()
                        tt(qd, qn[:, cs], eg, ALU.mult, [qn_r, eg_r], [qd_r])
                        d["at"] = (at, at_r)
                        d["qd"] = (qd, qd_r)
                    t1, t1_r = nextSF()
                    stt(t1, pb[bbe][:, 0:128], -1.0, dec, ALU.mult, ALU.mult, [pr[bbe], dec_r], [t1_r])
                    B0, B0_r = nextSF()
                    tt(B0, t1, pb[bkk][:, 0:128], ALU.mult, [t1_r, pr[bkk]], [B0_r])
                    tt(B0, B0, coffd, ALU.mult, [B0_r, cst_r], [B0_r])
                    A0, A0_r = nextSF()
                    stt(A0, pb[bkk][:, 0:128], gt[:, G_NBETA, j, h:h + 1], dec2, ALU.mult, ALU.mult,
                        [pr[bkk], gt_r, dec2_r], [A0_r])
                    Q, Q_r = nextSF()
                    tt(Q, B0, cident, ALU.add, [B0_r, cst_r], [Q_r])
                    d["A"] = (A0, A0_r)
                    d["B"] = (B0, B0_r)
                    d["Q"] = (Q, Q_r)
                    stt_[j] = d
                if sub < 3:
                    continue
                for lvl in range(1, 7):
                    for j in js:
                        d = stt_[j]
                        A, A_r = d["A"]
                        B, B_r = d["B"]
                        ba = nextbank()
                        mm(pb[ba][:, 0:128], B, A, [B_r, A_r], [pr[ba]])
                        if lvl < 6:
                            bb_ = nextbank()
                            mm(pb[bb_][:, 0:128], A, B, [A_r, B_r], [pr[bb_]])
                        An, An_r = nextSF()
                        cp(An, pb[ba][:, 0:128], [pr[ba]], [An_r], eng="act")
                        d["A"] = (An, An_r)
                        if lvl < 6:
                            Bn, Bn_r = nextSF()
                            cp(Bn, pb[bb_][:, 0:128], [pr[bb_]], [Bn_r], eng="dve")
                            d["B"] = (Bn, Bn_r)
                    for j in js:
                        d = stt_[j]
                        A, A_r = d["A"]
                        Q, Q_r = d["Q"]
                        bq = nextbank()
                        mm(pb[bq][:, 0:128], A, Q, [A_r, Q_r], [pr[bq]])
                        if lvl < 6:
                            Qn, Qn_r = nextSF()
                        else:
                            Qn, Qn_r = nextSB()
                        tt(Qn, Q, pb[bq][:, 0:128], ALU.add, [Q_r, pr[bq]], [Qn_r])
                        d["Q"] = (Qn, Qn_r)
                if sub < 4:
                    continue
                for j in js:
                    d = stt_[j]
                    TT, TT_r = d["Q"]
                    bu_ = nextbank()
                    mm(pb[bu_][:, 0:128], TT, ktok[:, 2, j, :], [TT_r, ktok_r[2]], [pr[bu_]])
                    bw_ = nextbank()
                    mm(pb[bw_][:, 0:128], ktok[:, 0, j, :], TT, [ktok_r[0], TT_r], [pr[bw_]])
                    u, u_r = nextSF()
                    cp(u, pb[bu_][:, 0:128], [pr[bu_]], [u_r], eng="act")
                    wT, wT_r = nextSB()
                    cp(wT, pb[bw_][:, 0:128], [pr[bw_]], [wT_r], eng="dve")
                    d["u"] = (u, u_r)
                    d["wT"] = (wT, wT_r)
                    prep[j] = d
                if sub < 5:
                    continue
                for j in js:
                    d = prep[j]
                    cs = slice(j * 128, (j + 1) * 128)
                    u, u_r = d["u"]
                    wT, wT_r = d["wT"]
                    b1 = nextbank()
                    mm(pb[b1][:, 0:128], wT, Sgb[:, h, :], [wT_r, Sgb_r[h]], [pr[b1]])
                    vn, vn_r = nextSB()
                    tt(vn, u, pb[b1][:, 0:128], ALU.subtract, [u_r, pr[b1]], [vn_r])
                    if main:
                        at, at_r = d["at"]
                        qd, qd_r = d["qd"]
                        bo = nextbank()
                        mm(pb[bo][:, 0:128], Sgb[:, h, :], qd, [Sgb_r[h], qd_r], [pr[bo]], start=True, stop=False)
                        mm(pb[bo][:, 0:128], vn, at, [vn_r, at_r], [pr[bo]], start=False, stop=True)
                        cp(osb[:, cs], pb[bo][:, 0:128], [pr[bo]], [osb_r], eng="act")
                    bd = nextbank()
                    mm(pb[bd][:, 0:128], ktok[:, 1, j, :], vn, [ktok_r[1], vn_r], [pr[bd]])
                    stt(Sgb[:, h, :], Sg[:, h, :], gt[:, G_ECD, j, h:h + 1], pb[bd][:, 0:128], ALU.mult, ALU.add,
                        [Sg_r[h], gt_r, pr[bd]], [Sgb_r[h]])
                    stt(Sg[:, h, :], Sg[:, h, :], gt[:, G_ECD, j, h:h + 1], pb[bd][:, 0:128], ALU.mult, ALU.add,
                        [Sg_r[h], gt_r, pr[bd]], [Sg_r[h]])
            if sub < 6:
                return
            if main:
                sq, sq_r = nextF()
                act(sq[:, 0:TB], osb[:], AF.Square, [osb_r], [sq_r])
                b2 = nextbank()
                mm(pb[b2][:], cones, sq[:, 0:TB], [cst_r, sq_r], [pr[b2]])
                rs, rs_r = nextF()
                rsqrt_ln(rs[:, 0:TB], pb[b2][:], [pr[b2], epsc_r], [rs_r], 1.0 / 128)
                t, t_r = nextF()
                stt(t[:, 0:TB], osb[:], gnw[:, 0:1], rs[:, 0:TB], ALU.mult, ALU.mult, [osb_r, gnw_r, rs_r], [t_r])
                tt(big[:, 8 + h, :], t[:, 0:TB], zs[:], ALU.mult, [t_r, zs_r], [big_r[8 + h]])

        def tap_mix(k0, k1):
            for k in range(k0, k1):
                f32t, f32t_r = nextF()
                cp(f32t[:, 0:TB], big[:, k, :], [big_r[k]], [f32t_r])
                tap("mix_%d" % k, f32t[:, 0:TB], [f32t_r])

        out_toks = []
        for blk in range(NBLK):
            main = blk >= NPRE
            load_x(blk)
            if stage < 1:
                continue
            norm_to_hT(0)
            if stage < 2:
                continue
            ffn(w1g, w1u, w1d, "f1")
            if blk == NPRE:
                for j in range(4):
                    tap("x1_%d" % j, xres[:, j, :], [xr[j]])
            if stage < 3:
                continue
            norm_to_hT(1)
            rotary_tables(blk)
            if stage < 4:
                continue
            for h in range(NH):
                ret_head(h, main)
            if blk == NPRE and main:
                tap_mix(0, 8)
            if stage < 5:
                continue
            gdn_gates()
            if stage < 6:
                continue
            for h in range(NH):
                gdn_head(h, main, qhist=(blk == NPRE - 1))
            if blk == NPRE and main:
                tap_mix(8, 16)
            if stage < 7:
                continue
            if main:
                rowproj(big_r, big, 16, wout, 1.0, "wo")
                norm_to_hT(2)
                ffn(w2g, w2u, w2d, "f2")
                rms_stats()
                for j in range(4):
                    stt(xres[:, j, :], xres[:, j, :], rstd[:, j:j + 1], gfin[:], ALU.mult, ALU.mult,
                        [xr[j], rstd_r, gfin_r], [xr[j]])
                    o0 = (blk - NPRE) * TB + j * 128
                    tk = P.dma("sp", "o%d" % j, lambda e, j=j, o0=o0: e.dma_start(out=out[o0:o0 + 128, :], in_=xres[:, j, :]),
                               [xr[j]], ())
                    out_toks.append(tk)
        P.wait_all("sp", out_toks + wo_toks)
        P.emit(st)
    return nc


_W_NAMES = ["norm_ffn1_w", "ffn1_w_gate", "ffn1_w_up", "ffn1_w_down", "norm_mix_w", "w_in", "conv_w",
            "gdn_a_log", "gdn_dt_bias", "gdn_norm_w", "w_out", "norm_ffn2_w", "ffn2_w_gate", "ffn2_w_up",
            "ffn2_w_down"]


def _weights_map(inputs):
    m = {}
    for n in _W_NAMES:
        a = np.asarray(inputs[n])
        m[n] = np.ascontiguousarray(a.reshape(a.shape[1:]))
    m["norm_final_w"] = np.ascontiguousarray(np.asarray(inputs["norm_final_w"]))
    cst, _ = _consts()
    m["consts"] = cst
    return m


def kernel(**inputs):
    x = np.asarray(inputs["x"])
    positions = np.asarray(inputs["positions"]).astype(np.int32)
    B, S, _ = x.shape
    half = S // 2
    nblk = half // TB
    wm = _weights_map(inputs)
    nc = build(nblk, nblk)
    in_maps = []
    for c in range(8):
        b, s = c // 2, c % 2
        xc = np.zeros((S, D), np.float32)
        pc = np.zeros((S,), np.int32)
        if s == 1:
            xc[:half] = x[b, :half]
            pc[:half] = positions[b, :half]
        xc[half:] = x[b, s * half:(s + 1) * half]
        pc[half:] = positions[b, s * half:(s + 1) * half]
        m = dict(wm)
        m["x"] = xc
        m["pos"] = pc
        in_maps.append(m)
    res = run_bass_kernel_spmd(nc, in_maps, core_ids=list(range(8)))
    outp = np.empty((B, S, D), np.float32)
    for c in range(8):
        b, s = c // 2, c % 2
        outp[b, s * half:(s + 1) * half] = res.results[c]["out"]
    return outp
```

```python
from contextlib import ExitStack
import numpy as np
import concourse.bass as bass
import concourse.mybir as mybir
from concourse.bass_utils import run_bass_kernel_spmd

F32 = mybir.dt.float32
BF16 = mybir.dt.bfloat16
I32 = mybir.dt.int32
AF = mybir.ActivationFunctionType
ALU = mybir.AluOpType

COMPUTE = ("pe", "act", "dve", "pool")
ALLQ = ("pe", "act", "dve", "pool", "sp")

D = 2048
DFF = 5632
NFT = DFF // 128
HD = 128
NH = 8
NIN = 8208
TB = 512
EPS = 1e-6
TWO_PI = 6.283185307179586
PI = 3.141592653589793


class Res:
    __slots__ = ("name", "lw", "rd")

    def __init__(self, name):
        self.name = name
        self.lw = None
        self.rd = []


class Ins:
    __slots__ = ("q", "fn", "deps", "inc", "tok", "dma_sem")

    def __init__(self, q, fn, deps, tok, dma_sem=None):
        self.q = q
        self.fn = fn
        self.deps = deps
        self.inc = False
        self.tok = tok
        self.dma_sem = dma_sem


class Prog:
    def __init__(self, nc):
        self.nc = nc
        self.q = {e: [] for e in ALLQ}
        self.ins_by_tok = {}
        self.dma_counts = {}

    def _deps(self, r, w):
        deps = set()
        for res in r:
            if res.lw is not None:
                deps.add(res.lw)
        for res in w:
            if res.lw is not None:
                deps.add(res.lw)
            deps.update(res.rd)
        return deps

    def _commit(self, tok, r, w):
        for res in r:
            res.rd.append(tok)
        for res in w:
            res.lw = tok
            res.rd = []

    def _mark(self, deps):
        for d in deps:
            if d[0] in COMPUTE:
                self.ins_by_tok[d].inc = True

    def op(self, eng, fn, r=(), w=()):
        deps = self._deps(r, w)
        lst = self.q[eng]
        tok = (eng, len(lst))
        if eng == "pe":
            deps = {d for d in deps if d[0] != "pe"}
        ins = Ins(eng, fn, deps, tok)
        lst.append(ins)
        self.ins_by_tok[tok] = ins
        self._mark(deps)
        self._commit(tok, r, w)
        return tok

    def dma(self, q, sem, fn, r=(), w=()):
        deps = self._deps(r, w)
        cnt = self.dma_counts.get(sem, 0) + 1
        self.dma_counts[sem] = cnt
        tok = ("dma:" + sem, cnt)
        ins = Ins(q, fn, deps, tok, dma_sem=sem)
        self.q[q].append(ins)
        self._mark(deps)
        self._commit(tok, r, w)
        return tok

    def wait_all(self, q, toks):
        deps = set(toks)
        tok = (q, len(self.q[q]))
        ins = Ins(q, None, deps, tok)
        if q in COMPUTE:
            self.ins_by_tok[tok] = ins
        self.q[q].append(ins)
        self._mark(deps)

    def emit(self, stack):
        nc = self.nc
        sems = {}
        for e in COMPUTE:
            sems[e] = stack.enter_context(nc.semaphore("s_" + e))
        for name in self.dma_counts:
            sems["dma:" + name] = stack.enter_context(nc.semaphore("d_" + name))
        val = {}
        for e in COMPUTE:
            c = 0
            for ins in self.q[e]:
                if ins.dma_sem is not None or ins.fn is None:
                    continue
                if ins.inc:
                    c += 1
                    val[ins.tok] = c
        block = stack.enter_context(nc.Block())
        attr = {"pe": "tensor", "act": "scalar", "dve": "vector", "pool": "gpsimd", "sp": "sync"}

        def make(qname):
            lst = self.q[qname]

            def body(eng):
                waited = {}
                for ins in lst:
                    need = {}
                    for d in ins.deps:
                        v = val[d] if d[0] in COMPUTE else 16 * d[1]
                        if v > need.get(d[0], 0):
                            need[d[0]] = v
                    for k, v in need.items():
                        if v > waited.get(k, 0):
                            eng.wait_ge(sems[k], v)
                            waited[k] = v
                    if ins.fn is None:
                        continue
                    r = ins.fn(eng)
                    if ins.dma_sem is not None:
                        r.then_inc(sems["dma:" + ins.dma_sem], 16)
                    elif ins.inc:
                        r.then_inc(sems[ins.tok[0]], 1)
            return body

        for qname in ALLQ:
            if self.q[qname]:
                getattr(block, attr[qname])(make(qname))


C_IDENT = 0
C_ONES = 128
C_U = 256
C_NEGT = 384
C_OFFD = 512
C_PERM = 640
C_CM = 768
C_NEG2 = 896
C_RDM = 1024
C_RQD = C_RDM + 1024
C_RKD = C_RQD + 1024
C_INVF = C_RKD + 8
C_TOT = C_INVF + 1


def _consts():
    c = np.zeros((128, C_TOT), np.float32)
    idx = np.arange(128)
    c[:, C_IDENT:C_IDENT + 128] = np.eye(128)
    c[:, C_ONES:C_ONES + 128] = 1.0
    s = idx[:, None]
    cc = idx[None, :]
    c[:, C_U:C_U + 128] = (s <= cc)
    c[:, C_NEGT:C_NEGT + 128] = np.where(cc >= s, 0.0, -1e30)
    c[:, C_OFFD:C_OFFD + 128] = 1.0 - np.eye(128)
    c[:, C_NEG2:C_NEG2 + 128] = np.where(s > cc, 0.0, -1e30)
    perm = np.zeros((128, 128), np.float32)
    for i in range(64):
        perm[i + 64, i] = -1.0
        perm[i, i + 64] = 1.0
    c[:, C_PERM:C_PERM + 128] = perm
    c[:, C_CM:C_CM + 128] = np.eye(128) - 1.0 / 128
    lg = np.log(1.0 - np.exp2(-5.0 - np.arange(NH, dtype=np.float64)))
    sc = HD ** -0.5
    for h in range(NH):
        rel = (cc - s).astype(np.float64)
        c[:, C_RDM + h * 128:C_RDM + (h + 1) * 128] = np.where(cc >= s, np.exp(lg[h] * np.maximum(rel, 0)) * sc, 0.0)
        c[:, C_RQD + h * 128:C_RQD + (h + 1) * 128] = np.exp(lg[h] * (idx + 1.0))[None, :]
        c[:, C_RKD + h] = np.exp(lg[h] * (127.0 - idx)) * sc
    half = 64
    invf = (10000.0 ** (-(np.arange(half, dtype=np.float32)) / half)).astype(np.float32)
    c[:, C_INVF] = np.concatenate([invf, invf])
    gam128 = [float(np.exp(lg[h] * 128.0)) for h in range(NH)]
    return c, gam128


def build(NPRE, NMAIN, taps=(), stage=99, sub=99):
    nc = bass.Bass("TRN2", target_bir_lowering=False)
    NBLK = NPRE + NMAIN
    SEQ = NBLK * TB
    _, GAM128 = _consts()

    def din(name, shape, dt=F32):
        return nc.dram_tensor(name, shape, dt, kind="ExternalInput").ap()

    x = din("x", [SEQ, D])
    pos = din("pos", [SEQ], I32)
    consts_d = din("consts", [128, C_TOT])
    g1_d = din("norm_ffn1_w", [D])
    gm_d = din("norm_mix_w", [D])
    g2_d = din("norm_ffn2_w", [D])
    gf_d = din("norm_final_w", [D])
    w1g = din("ffn1_w_gate", [D, DFF])
    w1u = din("ffn1_w_up", [D, DFF])
    w1d = din("ffn1_w_down", [DFF, D])
    w2g = din("ffn2_w_gate", [D, DFF])
    w2u = din("ffn2_w_up", [D, DFF])
    w2d = din("ffn2_w_down", [DFF, D])
    win = din("w_in", [D, NIN])
    wout = din("w_out", [D, D])
    convw = din("conv_w", [4, 3 * 1024])
    alog_d = din("gdn_a_log", [NH])
    dtb_d = din("gdn_dt_bias", [NH])
    gnw_d = din("gdn_norm_w", [HD])
    out = nc.dram_tensor("out", [NMAIN * TB, D], F32, kind="ExternalOutput").ap()
    tap_d = {}
    for (tn, shape) in taps:
        tap_d[tn] = nc.dram_tensor("tap_" + tn, list(shape), F32, kind="ExternalOutput").ap()

    def dscr(name, n, width):
        return nc.dram_tensor(name, [n, 128, width], BF16, kind="Internal").ap()

    SC = {
        "f1g": (dscr("sc_f1g", 22, 4096), [Res("sc_f1g%d" % i) for i in range(22)], set()),
        "f1u": (dscr("sc_f1u", 22, 4096), [Res("sc_f1u%d" % i) for i in range(22)], set()),
        "f1d": (dscr("sc_f1d", 44, 2048), [Res("sc_f1d%d" % i) for i in range(44)], set()),
        "f2g": (dscr("sc_f2g", 22, 4096), [Res("sc_f2g%d" % i) for i in range(22)], set()),
        "f2u": (dscr("sc_f2u", 22, 4096), [Res("sc_f2u%d" % i) for i in range(22)], set()),
        "f2d": (dscr("sc_f2d", 44, 2048), [Res("sc_f2d%d" % i) for i in range(44)], set()),
        "win": (dscr("sc_win", 64, 2048), [Res("sc_win%d" % i) for i in range(64)], set()),
        "wo": (dscr("sc_wo", 16, 2048), [Res("sc_wo%d" % i) for i in range(16)], set()),
    }

    st = ExitStack()
    with st:
        P = Prog(nc)

        def sb(name, shape, dt):
            return st.enter_context(nc.sbuf_tensor(name, shape, dt))

        def mm(out_, lhsT, rhs, r, w, start=True, stop=True):
            P.op("pe", lambda e: e.matmul(out_, lhsT=lhsT, rhs=rhs, start=start, stop=stop), r, w)

        def tr(out_, in_, ident, r, w):
            P.op("pe", lambda e: e.transpose(out_, in_, ident), r, w)

        def act(out_, in_, func, r, w, bias=None, scale=None, accum=None):
            kw = {}
            if bias is not None:
                kw["bias"] = bias
            if scale is not None:
                kw["scale"] = scale
            if accum is not None:
                kw["accum_out"] = accum
            P.op("act", lambda e: e.activation(out=out_, in_=in_, func=func, **kw), r, w)

        def tt(out_, in0, in1, op, r, w, eng="dve"):
            P.op(eng, lambda e: e.tensor_tensor(out=out_, in0=in0, in1=in1, op=op), r, w)

        def ts(out_, in0, s1, op0, r, w, s2=None, op1=None, eng="dve"):
            if op1 is None:
                P.op(eng, lambda e: e.tensor_scalar(out=out_, in0=in0, scalar1=s1, scalar2=None, op0=op0), r, w)
            else:
                P.op(eng, lambda e: e.tensor_scalar(out=out_, in0=in0, scalar1=s1, scalar2=s2, op0=op0, op1=op1), r, w)

        def stt(out_, in0, scalar, in1, op0, op1, r, w):
            P.op("dve", lambda e: e.scalar_tensor_tensor(out=out_, in0=in0, scalar=scalar, in1=in1, op0=op0, op1=op1), r, w)

        def cp(out_, in_, r, w, eng="dve"):
            if eng == "act":
                act(out_, in_, AF.Copy, r, w)
            else:
                P.op(eng, lambda e: e.tensor_copy(out=out_, in_=in_), r, w)

        def memset(ap, v, w):
            P.op("dve", lambda e: e.memset(ap, v), (), w)

        cnt = {"dma": 0}
        wo_toks = []

        def dma_small(q, out_, in_, w, r=()):
            cnt["dma"] += 1
            nm = "m%d" % cnt["dma"]
            return P.dma(q, nm, lambda e: e.dma_start(out=out_, in_=in_, allow_slow_non_contiguous=True), r, w)

        pb = [st.enter_context(nc.psum_tensor("pb%d" % i, [128, 512], F32)) for i in range(8)]
        pbb = [pb[i][:].bitcast(BF16) for i in range(8)]
        pr = [Res("pb%d" % i) for i in range(8)]
        bank_ctr = [0]

        def nextbank():
            i = bank_ctr[0] % 8
            bank_ctr[0] += 1
            return i

        cst = sb("cst", [128, 1033], F32); cst_r = Res("cst")
        cstb = sb("cstb", [128, 2048], BF16)
        identb = sb("identb", [128, 128], BF16); identb_r = Res("identb")
        xres = sb("xres", [128, 4, D], F32); xr = [Res("xres%d" % j) for j in range(4)]
        hT = sb("hT", [128, 16, TB], BF16); hT_r = [Res("hT%d" % k) for k in range(16)]
        big = sb("big", [128, NFT, TB], BF16); big_r = [Res("big%d" % k) for k in range(NFT)]

        class _XS:
            def __getitem__(self, idx):
                _, j, fs = idx
                return big[:, 4 * j:4 * j + 4, :].rearrange("p a b -> p (a b)")[:, fs]
        xs = _XS()
        xs_rl = [[big_r[4 * j + i] for i in range(4)] for j in range(4)]
        NRA = 5
        RA = [sb("ra%d" % i, [128, 16, 256], BF16) for i in range(NRA)]; ra_r = [Res("ra%d" % i) for i in range(NRA)]
        NRB = 4
        RB = [sb("rb%d" % i, [128, 2, 1024], BF16) for i in range(NRB)]; rb_r = [Res("rb%d" % i) for i in range(NRB)]
        ra_ctr = [0]
        rb_ctr = [0]
        gcols = sb("gcols", [128, 3, 16], F32); gcols_r = Res("gcols")
        gfin = sb("gfin", [128, D], F32); gfin_r = Res("gfin")
        ss = sb("ss", [128, 4], F32); ss_r = Res("ss")
        rstd = sb("rstd", [128, 4], F32); rstd_r = Res("rstd")
        sg = [sb("sg%d" % i, [128, TB], BF16) for i in range(2)]; sg_r = [Res("sg%d" % i) for i in range(2)]
        Sret = sb("Sret", [128, NH, 128], F32); Sret_r = [Res("Sret%d" % h) for h in range(NH)]
        Sretb = sb("Sretb", [128, NH, 128], BF16); Sretb_r = [Res("Sretb%d" % h) for h in range(NH)]
        Sg = sb("Sg", [128, NH, 128], F32); Sg_r = [Res("Sg%d" % h) for h in range(NH)]
        Sgb = sb("Sgb", [128, NH, 128], BF16); Sgb_r = [Res("Sgb%d" % h) for h in range(NH)]
        hist = sb("hist", [128, 24, 4], F32); hist_r = [Res("hist%d" % g) for g in range(24)]
        cwT = sb("cwT", [128, 24, 4], F32); cwT_r = Res("cwT")
        arow = sb("arow", [128, 3, NH], F32); arow_r = Res("arow")
        gnw = sb("gnw", [128, 1], F32); gnw_r = Res("gnw")
        cosT = sb("cosT", [128, TB], F32); cosT_r = Res("cosT")
        sinT = sb("sinT", [128, TB], F32); sinT_r = Res("sinT")
        NF = 5
        Fs = [sb("F%d" % i, [128, TB + 4], F32) for i in range(NF)]; Fs_r = [Res("F%d" % i) for i in range(NF)]
        f_ctr = [0]

        def nextF():
            i = f_ctr[0] % NF
            f_ctr[0] += 1
            return Fs[i], Fs_r[i]

        big_alias = {}
        SF_t, SF_r = [], []
        for k in range(16, 36):
            v = big[:, k, :].bitcast(F32)
            big_alias[k] = []
            for t in range(2):
                SF_t.append(v[:, t * 128:(t + 1) * 128])
                rr = Res("sfb%d_%d" % (k, t))
                SF_r.append(rr)
                big_alias[k].append(rr)
        NSF = len(SF_t)
        sf_ctr = [0]

        def nextSF():
            i = sf_ctr[0] % NSF
            sf_ctr[0] += 1
            return SF_t[i], SF_r[i]

        SB_t, SB_r = [], []
        for k in range(36, 44):
            big_alias[k] = []
            for t in range(4):
                SB_t.append(big[:, k, t * 128:(t + 1) * 128])
                rr = Res("sbb%d_%d" % (k, t))
                SB_r.append(rr)
                big_alias[k].append(rr)
        NSB = len(SB_t)
        sbt_ctr = [0]

        def nextSB():
            i = sbt_ctr[0] % NSB
            sbt_ctr[0] += 1
            return SB_t[i], SB_r[i]

        qn = sb("qn", [128, TB], BF16); qn_r = Res("qn")
        kn = sb("kn", [128, TB], BF16); kn_r = Res("kn")
        vsb = sb("vsb", [128, TB], BF16); vsb_r = Res("vsb")
        zs = sb("zs", [128, TB], F32); zs_r = Res("zs")
        ktok = sb("ktok", [128, 3, 4, 128], BF16); ktok_r = [Res("ktok%d" % i) for i in range(3)]
        osb = sb("osb", [128, TB], F32); osb_r = Res("osb")
        qdec = sb("qdec", [128, TB], BF16); qdec_r = Res("qdec")
        GN = 11
        gt = sb("gt", [128, GN, 4, NH], F32); gt_r = Res("gt")
        G_A, G_B, G_G, G_BETA, G_GC, G_NGC, G_EGC, G_EKD, G_ECD, G_BG, G_NBETA = range(GN)
        gtmp = sb("gtmp", [128, 4, 4, NH], F32); gtmp_r = Res("gtmp")

        dma_small("sp", cst[:, 0:1024], consts_d[:, 0:1024], [cst_r])
        dma_small("sp", cst[:, 1024:1033], consts_d[:, C_RKD:C_TOT], [cst_r])
        dma_small("pool", cstb[:], consts_d[:, C_RDM:C_RDM + 2048], [cst_r])
        cident = cst[:, C_IDENT:C_IDENT + 128]
        cones = cst[:, C_ONES:C_ONES + 128]
        cU = cst[:, C_U:C_U + 128]
        cnegt = cst[:, C_NEGT:C_NEGT + 128]
        coffd = cst[:, C_OFFD:C_OFFD + 128]
        cperm = cst[:, C_PERM:C_PERM + 128]
        ccm = cst[:, C_CM:C_CM + 128]
        cinvf = cst[:, 1032:1033]
        cneg2 = cst[:, C_NEG2:C_NEG2 + 128]
        cp(identb[:], cident, [cst_r], [identb_r])
        for i, gd in enumerate((g1_d, gm_d, g2_d)):
            dma_small("sp", gcols[:, i, :], gd.rearrange("(kt p) -> p kt", p=128), [gcols_r])
        dma_small("sp", gfin[:], gf_d.partition_broadcast(128), [gfin_r])
        for j in range(4):
            dma_small("sp", cwT[:, :, j], convw[j, :].rearrange("(g p) -> p g", p=128), [cwT_r])
        dma_small("sp", arow[:, 2, :], alog_d.partition_broadcast(128), [arow_r])
        dma_small("sp", arow[:, 1, :], dtb_d.partition_broadcast(128), [arow_r])
        dma_small("sp", gnw[:], gnw_d.rearrange("(p o) -> p o", o=1), [gnw_r])
        wabf = Fs[0][:, 0:256].rearrange("p (a b) -> p a b", a=16)
        wab = sb("wab", [128, 16, 16], BF16); wab_r = Res("wab")
        dma_small("sp", wabf, win.rearrange("(kt p) n -> p kt n", p=128)[:, :, 8192:8208], [Fs_r[0]])
        cp(wab[:], wabf, [Fs_r[0]], [wab_r])
        act(arow[:, 0, :], arow[:, 2, :], AF.Exp, [arow_r], [arow_r])
        ts(arow[:, 0, :], arow[:, 0, :], -1.0, ALU.mult, [arow_r], [arow_r])
        for h in range(NH):
            memset(Sret[:, h, :], 0.0, [Sret_r[h]])
            memset(Sretb[:, h, :], 0.0, [Sretb_r[h]])
            memset(Sg[:, h, :], 0.0, [Sg_r[h]])
            memset(Sgb[:, h, :], 0.0, [Sgb_r[h]])
        for g in range(24):
            memset(hist[:, g, :], 0.0, [hist_r[g]])

        def tap(name, ap, r):
            if name in tap_d:
                dma_small("sp", tap_d[name], ap, [], r=r)

        def load_x(blk):
            t0 = blk * TB
            for j in range(4):
                P.dma("sp", "x%d" % j,
                      lambda e, j=j: e.dma_start(out=xres[:, j, :], in_=x[t0 + j * 128:t0 + (j + 1) * 128, :]),
                      (), [xr[j]])

        def rms_stats():
            for j in range(4):
                act(xs[:, j, :], xres[:, j, :], AF.Square, [xr[j]], xs_rl[j] + [ss_r], accum=ss[:, j:j + 1])
            ts(rstd[:], ss[:], 1.0 / D, ALU.mult, [ss_r], [rstd_r], s2=EPS, op1=ALU.add)
            act(rstd[:], rstd[:], AF.Ln, [rstd_r], [rstd_r])
            act(rstd[:], rstd[:], AF.Exp, [rstd_r], [rstd_r], scale=-0.5)

        def norm_to_hT(gi):
            rms_stats()
            for j in range(4):
                ts(xs[:, j, :], xres[:, j, :], rstd[:, j:j + 1], ALU.mult, [xr[j], rstd_r], xs_rl[j])
            for kt in range(16):
                bk = nextbank()
                pv = pbb[bk]
                for j in range(4):
                    tr(pv[:, j * 128:(j + 1) * 128], xs[:, j, kt * 128:(kt + 1) * 128], identb[:],
                       xs_rl[j] + [identb_r], [pr[bk]])
                if kt % 2 == 0:
                    ts(hT[:, kt, :], pv[:, 0:TB], gcols[:, gi, kt:kt + 1], ALU.mult, [pr[bk], gcols_r], [hT_r[kt]])
                else:
                    act(hT[:, kt, :], pv[:, 0:TB], AF.Copy, [pr[bk], gcols_r], [hT_r[kt]], scale=gcols[:, gi, kt:kt + 1])

        def load_ra(src_ap, ncols, ck=None):
            i = ra_ctr[0] % NRA
            ra_ctr[0] += 1
            dst = RA[i][:, :, 0:ncols]
            if ck is None:
                P.dma("pool", "ra%d" % i, lambda e: e.dma_start(out=dst, in_=src_ap), (), [ra_r[i]])
                return i
            sct, resl, done = SC[ck[0]]
            idx = ck[1]
            scv = sct[idx, :, 0:16 * ncols].rearrange("p (k n) -> p k n", n=ncols)
            if idx not in done:
                done.add(idx)
                P.dma("pool", "ra%d" % i, lambda e: e.dma_start(out=dst, in_=src_ap), (), [ra_r[i]])
                wo_toks.append(P.dma("sp", "wra%d" % i, lambda e: e.dma_start(out=scv, in_=dst), [ra_r[i]], [resl[idx]]))
            else:
                P.dma("pool", "ra%d" % i, lambda e: e.dma_start(out=dst, in_=scv), [resl[idx]], [ra_r[i]])
            return i

        def colview(wap, c0, ncols):
            return wap.rearrange("(kt p) n -> p kt n", p=128)[:, :, c0:c0 + ncols]

        def rowproj(src_r, src_tile, nk, wd_ap, scale, ckn):
            wv = wd_ap.rearrange("(ft p) m -> p ft m", p=128)
            for half in range(2):
                sl = None
                for f in range(nk):
                    if f % 2 == 0:
                        sl = rb_ctr[0] % NRB
                        rb_ctr[0] += 1
                        sct, resl, done = SC[ckn]
                        idx = half * (nk // 2) + f // 2
                        scv = sct[idx, :, :].rearrange("p (k n) -> p k n", n=1024)
                        if idx not in done:
                            done.add(idx)
                            P.dma("pool", "rb%d" % sl,
                                  lambda e, sl=sl, f=f, half=half: e.dma_start(out=RB[sl][:], in_=wv[:, f:f + 2, half * 1024:(half + 1) * 1024]),
                                  (), [rb_r[sl]])
                            wo_toks.append(P.dma("sp", "wrb%d" % sl, lambda e, sl=sl, scv=scv: e.dma_start(out=scv, in_=RB[sl][:]),
                                                 [rb_r[sl]], [resl[idx]]))
                        else:
                            P.dma("pool", "rb%d" % sl, lambda e, sl=sl, scv=scv: e.dma_start(out=RB[sl][:], in_=scv),
                                  [resl[idx]], [rb_r[sl]])
                    for j in range(4):
                        for mb in range(2):
                            b = j * 2 + mb
                            mm(pb[b][:], src_tile[:, f, j * 128:(j + 1) * 128], RB[sl][:, f % 2, mb * 512:(mb + 1) * 512],
                               [src_r[f], rb_r[sl]] + (big_alias.get(f, []) if nk == NFT else []), [pr[b]], start=(f == 0), stop=(f == nk - 1))
                for j in range(4):
                    for mb in range(2):
                        b = j * 2 + mb
                        col = half * 1024 + mb * 512
                        stt(xres[:, j, col:col + 512], pb[b][:], float(scale), xres[:, j, col:col + 512],
                            ALU.mult, ALU.add, [pr[b], xr[j]], [xr[j]])

        def ffn(wg, wu, wd, pfx):
            for fg in range(NFT // 2):
                sa = load_ra(colview(wg, fg * 256, 256), 256, ck=(pfx + "g", fg))
                su = load_ra(colview(wu, fg * 256, 256), 256, ck=(pfx + "u", fg))
                for fo in range(2):
                    f = fg * 2 + fo
                    bg = nextbank()
                    bu = nextbank()
                    for kt in range(16):
                        mm(pb[bg][:], RA[sa][:, kt, fo * 128:(fo + 1) * 128], hT[:, kt, :], [ra_r[sa], hT_r[kt]], [pr[bg]],
                           start=(kt == 0), stop=(kt == 15))
                        mm(pb[bu][:], RA[su][:, kt, fo * 128:(fo + 1) * 128], hT[:, kt, :], [ra_r[su], hT_r[kt]], [pr[bu]],
                           start=(kt == 0), stop=(kt == 15))
                    act(sg[f % 2][:], pb[bg][:], AF.Silu, [pr[bg]], [sg_r[f % 2]])
                    tt(big[:, f, :], sg[f % 2][:], pb[bu][:], ALU.mult, [sg_r[f % 2], pr[bu]], [big_r[f]] + big_alias.get(f, []))
            rowproj(big_r, big, NFT, wd, 0.5, pfx + "d")

        def proj_fm(c0, blk_main=True):
            sl = load_ra(colview(win, c0, 128), 128, ck=("win", c0 // 128))
            bk = nextbank()
            for kt in range(16):
                mm(pb[bk][:], RA[sl][:, kt, 0:128], hT[:, kt, :], [ra_r[sl], hT_r[kt]], [pr[bk]], start=(kt == 0), stop=(kt == 15))
            return bk

        def rsqrt_ln(out_ap, in_ap, r, w, scale, bias_exp=None):
            act(out_ap, in_ap, AF.Ln, r, w, bias=epsc[:, 0:1], scale=scale)
            if bias_exp is None:
                act(out_ap, out_ap, AF.Exp, w, w, scale=-0.5)
            else:
                act(out_ap, out_ap, AF.Exp, list(w) + [epsc_r], w, scale=-0.5, bias=bias_exp)

        epsc = sb("epsc", [128, 4], F32); epsc_r = Res("epsc")
        memset(epsc[:, 0:1], EPS, [epsc_r])
        memset(epsc[:, 1:2], float(-0.5 * np.log(128.0)), [epsc_r])
        memset(epsc[:, 2:3], 1.0, [epsc_r])
        memset(epsc[:, 3:4], 0.0, [epsc_r])

        def rotary_tables(blk):
            t0 = blk * TB
            pbuf, posi_r = nextF()
            posi = pbuf[:, 0:TB].bitcast(I32)
            P.dma("sp", "pos", lambda e: e.dma_start(out=posi, in_=pos[t0:t0 + TB].partition_broadcast(128)), (), [posi_r])
            ang, ang_r = nextF()
            cp(ang[:, 0:TB], posi, [posi_r], [ang_r])
            ts(ang[:, 0:TB], ang[:, 0:TB], cinvf, ALU.mult, [ang_r, cst_r], [ang_r])
            for (dst, dst_r, shift) in ((sinT, sinT_r, 0.0), (cosT, cosT_r, PI / 2)):
                a2, a2_r = nextF()
                t, t_r = nextF()
                m, m_r = nextF()
                ts(a2[:, 0:TB], ang[:, 0:TB], float(shift), ALU.add, [ang_r], [a2_r])
                ts(t[:, 0:TB], a2[:, 0:TB], float(1.0 / TWO_PI), ALU.mult, [a2_r], [t_r])
                ki = m[:, 0:TB].bitcast(I32)
                cp(ki, t[:, 0:TB], [t_r], [m_r])
                cp(t[:, 0:TB], ki, [m_r], [t_r])
                stt(a2[:, 0:TB], t[:, 0:TB], float(-TWO_PI), a2[:, 0:TB], ALU.mult, ALU.add, [t_r, a2_r], [a2_r])
                ts(m[:, 0:TB], a2[:, 0:TB], float(PI), ALU.is_gt, [a2_r], [m_r])
                stt(a2[:, 0:TB], m[:, 0:TB], float(-TWO_PI), a2[:, 0:TB], ALU.mult, ALU.add, [m_r, a2_r], [a2_r])
                ts(m[:, 0:TB], a2[:, 0:TB], float(-PI), ALU.is_lt, [a2_r], [m_r])
                stt(a2[:, 0:TB], m[:, 0:TB], float(TWO_PI), a2[:, 0:TB], ALU.mult, ALU.add, [m_r, a2_r], [a2_r])
                ts(a2[:, 0:TB], a2[:, 0:TB], float(PI), ALU.min, [a2_r], [a2_r], s2=float(-PI), op1=ALU.max)
                act(dst[:], a2[:, 0:TB], AF.Sin, [a2_r], [dst_r])

        def ret_head(h, main):
            bk_k = proj_fm(1024 + h * 128)
            kf, kf_r = nextF()
            cp(kf[:, 0:TB], pb[bk_k][:], [pr[bk_k]], [kf_r], eng="act")
            if main:
                bk_q = proj_fm(h * 128)
                qf, qf_r = nextF()
                cp(qf[:, 0:TB], pb[bk_q][:], [pr[bk_q]], [qf_r], eng="act")
                bk_g = proj_fm(3072 + h * 128)
                act(zs[:], pb[bk_g][:], AF.Silu, [pr[bk_g]], [zs_r])
            sl = load_ra(colview(win, 2048 + h * 128, 128), 128, ck=("win", 16 + h))
            bk_v = nextbank()
            for j in range(4):
                for kt in range(16):
                    mm(pb[bk_v][:, j * 128:(j + 1) * 128], hT[:, kt, j * 128:(j + 1) * 128], RA[sl][:, kt, 0:128],
                       [ra_r[sl], hT_r[kt]], [pr[bk_v]], start=(kt == 0), stop=(kt == 15))
            vt = ktok[:, 1, :, :]
            cp(vt.rearrange("p a b -> p (a b)"), pb[bk_v][:], [pr[bk_v]], [ktok_r[1]])

            def rot(srcf, srcf_r, dst, dst_r):
                bk = nextbank()
                mm(pb[bk][:], cperm, srcf[:, 0:TB], [cst_r, srcf_r], [pr[bk]])
                t1, t1_r = nextF()
                t2, t2_r = nextF()
                tt(t1[:, 0:TB], srcf[:, 0:TB], cosT[:], ALU.mult, [srcf_r, cosT_r], [t1_r])
                tt(t2[:, 0:TB], pb[bk][:], sinT[:], ALU.mult, [pr[bk], sinT_r], [t2_r])
                tt(dst[:], t1[:, 0:TB], t2[:, 0:TB], ALU.add, [t1_r, t2_r], [dst_r])

            rot(kf, kf_r, kn, kn_r)
            if main:
                rot(qf, qf_r, qn, qn_r)
                tt(qdec[:].rearrange("p (a b) -> p a b", a=4), qn[:].rearrange("p (a b) -> p a b", a=4),
                   cstb[:, 1024 + h * 128:1024 + (h + 1) * 128].unsqueeze(1).broadcast_to([128, 4, 128]),
                   ALU.mult, [qn_r, cst_r], [qdec_r])
            bk = nextbank()
            for j in range(4):
                tr(pbb[bk][:, j * 128:(j + 1) * 128], kn[:, j * 128:(j + 1) * 128], identb[:], [kn_r, identb_r], [pr[bk]])
            ts(ktok[:, 0, :, :].rearrange("p a b -> p (a b)"), pbb[bk][:, 0:TB], cst[:, 1024 + h:1024 + h + 1], ALU.mult,
               [pr[bk], cst_r], [ktok_r[0]])
            for j in range(4):
                cs = slice(j * 128, (j + 1) * 128)
                if main:
                    bs = nextbank()
                    mm(pb[bs][:, 0:128], kn[:, cs], qn[:, cs], [kn_r, qn_r], [pr[bs]])
                    scm, scm_r = nextSB()
                    tt(scm, pb[bs][:, 0:128], cstb[:, h * 128:(h + 1) * 128], ALU.mult, [pr[bs], cst_r], [scm_r])
                    bo = nextbank()
                    mm(pb[bo][:, 0:128], ktok[:, 1, j, :], scm, [ktok_r[1], scm_r], [pr[bo]], start=True, stop=False)
                    mm(pb[bo][:, 0:128], Sretb[:, h, :], qdec[:, cs], [Sretb_r[h], qdec_r], [pr[bo]], start=False, stop=True)
                    cp(osb[:, cs], pb[bo][:, 0:128], [pr[bo]], [osb_r], eng="act")
                bd = nextbank()
                mm(pb[bd][:, 0:128], ktok[:, 0, j, :], ktok[:, 1, j, :], [ktok_r[0], ktok_r[1]], [pr[bd]])
                stt(Sret[:, h, :], Sret[:, h, :], float(GAM128[h]), pb[bd][:, 0:128], ALU.mult, ALU.add,
                    [Sret_r[h], pr[bd]], [Sret_r[h]])
                cp(Sretb[:, h, :], Sret[:, h, :], [Sret_r[h]], [Sretb_r[h]], eng="act")
            if main:
                bc_ = nextbank()
                mm(pb[bc_][:], ccm, osb[:], [cst_r, osb_r], [pr[bc_]])
                sq, sq_r = nextF()
                act(sq[:, 0:TB], pb[bc_][:], AF.Square, [pr[bc_]], [sq_r])
                b2 = nextbank()
                mm(pb[b2][:], cones, sq[:, 0:TB], [cst_r, sq_r], [pr[b2]])
                rs, rs_r = nextF()
                rsqrt_ln(rs[:, 0:TB], pb[b2][:], [pr[b2], epsc_r], [rs_r], 1.0 / 128)
                t, t_r = nextF()
                tt(t[:, 0:TB], pb[bc_][:], rs[:, 0:TB], ALU.mult, [pr[bc_], rs_r], [t_r])
                tt(big[:, h, :], t[:, 0:TB], zs[:], ALU.mult, [t_r, zs_r], [big_r[h]])

        def gdn_gates():
            bk = nextbank()
            for j in range(4):
                for kt in range(16):
                    mm(pb[bk][:, j * 16:(j + 1) * 16], hT[:, kt, j * 128:(j + 1) * 128], wab[:, kt, :],
                       [wab_r, hT_r[kt]], [pr[bk]], start=(kt == 0), stop=(kt == 15))
            pv = pb[bk][:, 0:64].rearrange("p (j c) -> p j c", j=4)
            cp(gt[:, G_A, :, :], pv[:, :, 0:8], [pr[bk]], [gt_r])
            cp(gt[:, G_B, :, :], pv[:, :, 8:16], [pr[bk]], [gt_r])
            bro = lambda row: arow[:, row, :].unsqueeze(1).broadcast_to([128, 4, NH])
            R = [gt_r, gtmp_r, arow_r]
            xa = gtmp[:, 0]
            ax = gtmp[:, 1]
            l = gtmp[:, 2]
            tt(xa, gt[:, G_A], bro(1), ALU.add, R, [gtmp_r])
            stt(ax, xa, -1.0, xa, ALU.mult, ALU.max, R, [gtmp_r])
            act(l, ax, AF.Exp, R, [gtmp_r], scale=-1.0)
            act(l, l, AF.Ln, R + [epsc_r], [gtmp_r], bias=epsc[:, 2:3])
            ts(ax, xa, 0.0, ALU.max, R, [gtmp_r])
            tt(l, l, ax, ALU.add, R, [gtmp_r])
            tt(gt[:, G_G], l, bro(0), ALU.mult, R, [gt_r])
            act(l, gt[:, G_B], AF.Exp, R, [gtmp_r], scale=-1.0)
            ts(l, l, 1.0, ALU.add, R, [gtmp_r])
            P.op("dve", lambda e: e.reciprocal(out=gt[:, G_BETA], in_=l), R, [gt_r])
            bk2 = nextbank()
            ggf = gt[:, G_G, :, :].rearrange("p j c -> p (j c)")
            mm(pb[bk2][:, 0:32], cU, ggf, [cst_r, gt_r], [pr[bk2]])
            mm(pb[bk2][:, 64:96], cones, ggf, [cst_r, gt_r], [pr[bk2]])
            gcv = pb[bk2][:, 0:32].rearrange("p (j c) -> p j c", j=4)
            glv = pb[bk2][:, 64:96].rearrange("p (j c) -> p j c", j=4)
            cp(gt[:, G_GC], gcv, [pr[bk2]], [gt_r])
            ts(gt[:, G_NGC], gcv, -1.0, ALU.mult, [pr[bk2]], [gt_r])
            act(gt[:, G_EGC], gcv, AF.Exp, [pr[bk2]], [gt_r])
            act(gt[:, G_ECD], glv, AF.Exp, [pr[bk2]], [gt_r])
            tt(l, glv, gt[:, G_GC], ALU.subtract, [pr[bk2], gt_r, gtmp_r], [gtmp_r])
            act(gt[:, G_EKD], l, AF.Exp, [gtmp_r], [gt_r])
            tt(gt[:, G_BG], gt[:, G_BETA], gt[:, G_EGC], ALU.mult, [gt_r], [gt_r])
            ts(gt[:, G_NBETA], gt[:, G_BETA], -1.0, ALU.mult, [gt_r], [gt_r])

        def gdn_head(h, main, qhist=False):
            def conv_path(qi, c0, dst, dst_r, norm, qscale=False):
                g = qi * NH + h
                bk = proj_fm(c0)
                xc, xc_r = nextF()
                cp(xc[:, 0:3], hist[:, g, 0:3], [hist_r[g]], [xc_r])
                cp(xc[:, 3:3 + TB], pb[bk][:], [pr[bk]], [xc_r], eng="act")
                cp(hist[:, g, 0:3], xc[:, TB:TB + 3], [xc_r], [hist_r[g]])
                if sub < 0.1:
                    return
                y, y_r = nextF()
                ts(y[:, 0:TB], xc[:, 3:3 + TB], cwT[:, g, 3:4], ALU.mult, [xc_r, cwT_r], [y_r])
                for jj in (2, 1, 0):
                    stt(y[:, 0:TB], xc[:, jj:jj + TB], cwT[:, g, jj:jj + 1], y[:, 0:TB], ALU.mult, ALU.add,
                        [xc_r, cwT_r, y_r], [y_r])
                if sub < 0.2:
                    return
                if not norm:
                    act(dst[:], y[:, 0:TB], AF.Silu, [y_r], [dst_r])
                    return
                act(y[:, 0:TB], y[:, 0:TB], AF.Silu, [y_r], [y_r])
                sq, sq_r = nextF()
                act(sq[:, 0:TB], y[:, 0:TB], AF.Square, [y_r], [sq_r])
                if sub < 0.3:
                    return
                b2 = nextbank()
                mm(pb[b2][:], cones, sq[:, 0:TB], [cst_r, sq_r], [pr[b2]])
                if sub < 0.4:
                    return
                rsqrt_ln(sq[:, 0:TB], pb[b2][:], [pr[b2], epsc_r], [sq_r], 1.0, bias_exp=(epsc[:, 1:2] if qscale else None))
                if sub < 0.5:
                    return
                tt(dst[:], y[:, 0:TB], sq[:, 0:TB], ALU.mult, [y_r, sq_r], [dst_r])

            conv_path(1, 5120 + h * 128, kn, kn_r, True)
            conv_path(2, 6144 + h * 128, vsb, vsb_r, False)
            if main or qhist:
                conv_path(0, 4096 + h * 128, qn, qn_r, True, qscale=True)
            if main:
                bz = proj_fm(7168 + h * 128)
                act(zs[:], pb[bz][:], AF.Silu, [pr[bz]], [zs_r])
            if sub < 1:
                return
            bk = nextbank()
            for j in range(4):
                tr(pbb[bk][:, j * 128:(j + 1) * 128], kn[:, j * 128:(j + 1) * 128], identb[:], [kn_r, identb_r], [pr[bk]])
            bkv = nextbank()
            for j in range(4):
                tr(pbb[bkv][:, j * 128:(j + 1) * 128], vsb[:, j * 128:(j + 1) * 128], identb[:], [vsb_r, identb_r], [pr[bkv]])
            for j in range(4):
                ksl = pbb[bk][:, j * 128:(j + 1) * 128]
                vsl = pbb[bkv][:, j * 128:(j + 1) * 128]
                ts(ktok[:, 0, j, :], ksl, gt[:, G_BG, j, h:h + 1], ALU.mult, [pr[bk], gt_r], [ktok_r[0]])
                ts(ktok[:, 1, j, :], ksl, gt[:, G_EKD, j, h:h + 1], ALU.mult, [pr[bk], gt_r], [ktok_r[1]])
                ts(ktok[:, 2, j, :], vsl, gt[:, G_BETA, j, h:h + 1], ALU.mult, [pr[bkv], gt_r], [ktok_r[2]])

            if sub < 2:
                return
            prep = {}
            for pair in range(1):
                js = (0, 1, 2, 3)
                stt_ = {}
                for j in js:
                    cs = slice(j * 128, (j + 1) * 128)
                    d = {}
                    bkk = nextbank()
                    mm(pb[bkk][:, 0:128], kn[:, cs], kn[:, cs], [kn_r], [pr[bkk]])
                    if main:
                        bqk = nextbank()
                        mm(pb[bqk][:, 0:128], kn[:, cs], qn[:, cs], [kn_r, qn_r], [pr[bqk]])
                    dg, dg_r = nextSF()
                    db, db_r = nextSF()
                    ts(dg, cU, gt[:, G_G, j, h:h + 1], ALU.mult, [cst_r, gt_r], [dg_r])
                    ts(db, cident, gt[:, G_BETA, j, h:h + 1], ALU.mult, [cst_r, gt_r], [db_r])
                    bgc = nextbank()
                    mm(pb[bgc][:, 0:128], cones, dg, [cst_r, dg_r], [pr[bgc]])
                    bbe = nextbank()
                    mm(pb[bbe][:, 0:128], cones, db, [cst_r, db_r], [pr[bbe]])
                    dec, dec_r = nextSF()
                    tt(dec, pb[bgc][:, 0:128], cnegt, ALU.add, [pr[bgc], cst_r], [dec_r])
                    act(dec, dec, AF.Exp, [dec_r, gt_r], [dec_r], bias=gt[:, G_NGC, j, h:h + 1])
                    dec2, dec2_r = nextSF()
                    stt(dec2, pb[bgc][:, 0:128], -1.0, cneg2, ALU.mult, ALU.add, [pr[bgc], cst_r], [dec2_r])
                    act(dec2, dec2, AF.Exp, [dec2_r, gt_r], [dec2_r], bias=gt[:, G_GC, j, h:h + 1])
                    if main:
                        at, at_r = nextSB()
                        tt(at, pb[bqk][:, 0:128], dec, ALU.mult, [pr[bqk], dec_r], [at_r])
                        eg, eg_r = nextSF()
                        act(eg, pb[bgc][:, 0:128], AF.Exp, [pr[bgc]], [eg_r])
                        qd, qd_r = nextSB()
                        tt(qd, qn[:, cs], eg, ALU.mult, [qn_r, eg_r], [qd_r])
                        d["at"] = (at, at_r)
                        d["qd"] = (qd, qd_r)
                    t1, t1_r = nextSF()
                    stt(t1, pb[bbe][:, 0:128], -1.0, dec, ALU.mult, ALU.mult, [pr[bbe], dec_r], [t1_r])
                    B0, B0_r = nextSF()
                    tt(B0, t1, pb[bkk][:, 0:128], ALU.mult, [t1_r, pr[bkk]], [B0_r])
                    tt(B0, B0, coffd, ALU.mult, [B0_r, cst_r], [B0_r])
                    A0, A0_r = nextSF()
                    stt(A0, pb[bkk][:, 0:128], gt[:, G_NBETA, j, h:h + 1], dec2, ALU.mult, ALU.mult,
                        [pr[bkk], gt_r, dec2_r], [A0_r])
                    Q, Q_r = nextSF()
                    tt(Q, B0, cident, ALU.add, [B0_r, cst_r], [Q_r])
                    d["A"] = (A0, A0_r)
                    d["B"] = (B0, B0_r)
                    d["Q"] = (Q, Q_r)
                    stt_[j] = d
                if sub < 3:
                    continue
                for lvl in range(1, 7):
                    for j in js:
                        d = stt_[j]
                        A, A_r = d["A"]
                        B, B_r = d["B"]
                        ba = nextbank()
                        mm(pb[ba][:, 0:128], B, A, [B_r, A_r], [pr[ba]])
                        if lvl < 6:
                            bb_ = nextbank()
                            mm(pb[bb_][:, 0:128], A, B, [A_r, B_r], [pr[bb_]])
                        An, An_r = nextSF()
                        cp(An, pb[ba][:, 0:128], [pr[ba]], [An_r], eng="act")
                        d["A"] = (An, An_r)
                        if lvl < 6:
                            Bn, Bn_r = nextSF()
                            cp(Bn, pb[bb_][:, 0:128], [pr[bb_]], [Bn_r], eng="dve")
                            d["B"] = (Bn, Bn_r)
                    for j in js:
                        d = stt_[j]
                        A, A_r = d["A"]
                        Q, Q_r = d["Q"]
                        bq = nextbank()
                        mm(pb[bq][:, 0:128], A, Q, [A_r, Q_r], [pr[bq]])
                        if lvl < 6:
                            Qn, Qn_r = nextSF()
                        else:
                            Qn, Qn_r = nextSB()
                        tt(Qn, Q, pb[bq][:, 0:128], ALU.add, [Q_r, pr[bq]], [Qn_r])
                        d["Q"] = (Qn, Qn_r)
                if sub < 4:
                    continue
                for j in js:
                    d = stt_[j]
                    TT, TT_r = d["Q"]
                    bu_ = nextbank()
                    mm(pb[bu_][:, 0:128], TT, ktok[:, 2, j, :], [TT_r, ktok_r[2]], [pr[bu_]])
                    bw_ = nextbank()
                    mm(pb[bw_][:, 0:128], ktok[:, 0, j, :], TT, [ktok_r[0], TT_r], [pr[bw_]])
                    u, u_r = nextSF()
                    cp(u, pb[bu_][:, 0:128], [pr[bu_]], [u_r], eng="act")
                    wT, wT_r = nextSB()
                    cp(wT, pb[bw_][:, 0:128], [pr[bw_]], [wT_r], eng="dve")
                    d["u"] = (u, u_r)
                    d["wT"] = (wT, wT_r)
                    prep[j] = d
                if sub < 5:
                    continue
                for j in js:
                    d = prep[j]
                    cs = slice(j * 128, (j + 1) * 128)
                    u, u_r = d["u"]
                    wT, wT_r = d["wT"]
                    b1 = nextbank()
                    mm(pb[b1][:, 0:128], wT, Sgb[:, h, :], [wT_r, Sgb_r[h]], [pr[b1]])
                    vn, vn_r = nextSB()
                    tt(vn, u, pb[b1][:, 0:128], ALU.subtract, [u_r, pr[b1]], [vn_r])
                    if main:
                        at, at_r = d["at"]
                        qd, qd_r = d["qd"]
                        bo = nextbank()
                        mm(pb[bo][:, 0:128], Sgb[:, h, :], qd, [Sgb_r[h], qd_r], [pr[bo]], start=True, stop=False)
                        mm(pb[bo][:, 0:128], vn, at, [vn_r, at_r], [pr[bo]], start=False, stop=True)
                        cp(osb[:, cs], pb[bo][:, 0:128], [pr[bo]], [osb_r], eng="act")
                    bd = nextbank()
                    mm(pb[bd][:, 0:128], ktok[:, 1, j, :], vn, [ktok_r[1], vn_r], [pr[bd]])
                    stt(Sg[:, h, :], Sg[:, h, :], gt[:, G_ECD, j, h:h + 1], pb[bd][:, 0:128], ALU.mult, ALU.add,
                        [Sg_r[h], gt_r, pr[bd]], [Sg_r[h]])
                    cp(Sgb[:, h, :], Sg[:, h, :], [Sg_r[h]], [Sgb_r[h]], eng="act")
            if sub < 6:
                return
            if main:
                sq, sq_r = nextF()
                act(sq[:, 0:TB], osb[:], AF.Square, [osb_r], [sq_r])
                b2 = nextbank()
                mm(pb[b2][:], cones, sq[:, 0:TB], [cst_r, sq_r], [pr[b2]])
                rs, rs_r = nextF()
                rsqrt_ln(rs[:, 0:TB], pb[b2][:], [pr[b2], epsc_r], [rs_r], 1.0 / 128)
                t, t_r = nextF()
                stt(t[:, 0:TB], osb[:], gnw[:, 0:1], rs[:, 0:TB], ALU.mult, ALU.mult, [osb_r, gnw_r, rs_r], [t_r])
                tt(big[:, 8 + h, :], t[:, 0:TB], zs[:], ALU.mult, [t_r, zs_r], [big_r[8 + h]])

        def tap_mix(k0, k1):
            for k in range(k0, k1):
                f32t, f32t_r = nextF()
                cp(f32t[:, 0:TB], big[:, k, :], [big_r[k]], [f32t_r])
                tap("mix_%d" % k, f32t[:, 0:TB], [f32t_r])

        out_toks = []
        for blk in range(NBLK):
            main = blk >= NPRE
            load_x(blk)
            if stage < 1:
                continue
            norm_to_hT(0)
            if stage < 2:
                continue
            ffn(w1g, w1u, w1d, "f1")
            if blk == NPRE:
                for j in range(4):
                    tap("x1_%d" % j, xres[:, j, :], [xr[j]])
            if stage < 3:
                continue
            norm_to_hT(1)
            rotary_tables(blk)
            if stage < 4:
                continue
            for h in range(NH):
                ret_head(h, main)
            if blk == NPRE and main:
                tap_mix(0, 8)
            if stage < 5:
                continue
            gdn_gates()
            if stage < 6:
                continue
            for h in range(NH):
                gdn_head(h, main, qhist=(blk == NPRE - 1))
            if blk == NPRE and main:
                tap_mix(8, 16)
            if stage < 7:
                continue
            if main:
                rowproj(big_r, big, 16, wout, 1.0, "wo")
                norm_to_hT(2)
                ffn(w2g, w2u, w2d, "f2")
                rms_stats()
                for j in range(4):
                    stt(xres[:, j, :], xres[:, j, :], rstd[:, j:j + 1], gfin[:], ALU.mult, ALU.mult,
                        [xr[j], rstd_r, gfin_r], [xr[j]])
                    o0 = (blk - NPRE) * TB + j * 128
                    tk = P.dma("sp", "o%d" % j, lambda e, j=j, o0=o0: e.dma_start(out=out[o0:o0 + 128, :], in_=xres[:, j, :]),
                               [xr[j]], ())
                    out_toks.append(tk)
        P.wait_all("sp", out_toks + wo_toks)
        P.emit(st)
    return nc


_W_NAMES = ["norm_ffn1_w", "ffn1_w_gate", "ffn1_w_up", "ffn1_w_down", "norm_mix_w", "w_in", "conv_w",
            "gdn_a_log", "gdn_dt_bias", "gdn_norm_w", "w_out", "norm_ffn2_w", "ffn2_w_gate", "ffn2_w_up",
            "ffn2_w_down"]


def _weights_map(inputs):
    m = {}
    for n in _W_NAMES:
        a = np.asarray(inputs[n])
        m[n] = np.ascontiguousarray(a.reshape(a.shape[1:]))
    m["norm_final_w"] = np.ascontiguousarray(np.asarray(inputs["norm_final_w"]))
    cst, _ = _consts()
    m["consts"] = cst
    return m


def kernel(**inputs):
    x = np.asarray(inputs["x"])
    positions = np.asarray(inputs["positions"]).astype(np.int32)
    B, S, _ = x.shape
    half = S // 2
    nblk = half // TB
    wm = _weights_map(inputs)
    nc = build(nblk, nblk)
    in_maps = []
    for c in range(8):
        b, s = c // 2, c % 2
        xc = np.zeros((S, D), np.float32)
        pc = np.zeros((S,), np.int32)
        if s == 1:
            xc[:half] = x[b, :half]
            pc[:half] = positions[b, :half]
        xc[half:] = x[b, s * half:(s + 1) * half]
        pc[half:] = positions[b, s * half:(s + 1) * half]
        m = dict(wm)
        m["x"] = xc
        m["pos"] = pc
        in_maps.append(m)
    res = run_bass_kernel_spmd(nc, in_maps, core_ids=list(range(8)))
    outp = np.empty((B, S, D), np.float32)
    for c in range(8):
        b, s = c // 2, c % 2
        outp[b, s * half:(s + 1) * half] = res.results[c]["out"]
    return outp
```
